# Optimizing a Trainium2 kernel written in Bass

```python
import math
import jax, jax.numpy as jnp
from jax import lax
import numpy as np

D_MODEL = 1024
BATCH = 2
SEQ = 8192
DEPTH = 1

ROPE_THETA = 500000.0
NORM_EPS = 1e-6
BLOCK_Q = 128
MLA_HEADS = 8
MLA_NOPE_DIM = 64
MLA_ROPE_DIM = 32
MLA_QK_DIM = MLA_NOPE_DIM + MLA_ROPE_DIM
MLA_V_DIM = 64
MLA_Q_RANK = 384
MLA_KV_RANK = 256
MLA_WIDTH = MLA_HEADS * MLA_V_DIM
DIFF_HEADS = 4
DIFF_HEAD_DIM = 64
DIFF_V_DIM = 2 * DIFF_HEAD_DIM
DIFF_ROT_DIM = DIFF_HEAD_DIM // 4
DIFF_QK_WIDTH = DIFF_HEADS * 2 * DIFF_HEAD_DIM
DIFF_WIDTH = DIFF_HEADS * DIFF_V_DIM
N_BRANCHES = 2
IN_SIZES = (MLA_Q_RANK, MLA_KV_RANK, MLA_ROPE_DIM, DIFF_QK_WIDTH, DIFF_QK_WIDTH, DIFF_WIDTH, N_BRANCHES * D_MODEL)
IN_WIDTH = MLA_Q_RANK + MLA_KV_RANK + MLA_ROPE_DIM + 2 * DIFF_QK_WIDTH + DIFF_WIDTH + N_BRANCHES * D_MODEL
FFN_HIDDEN = ((-(-8 * D_MODEL // 3) + 255) // 256) * 256

kernel_name = "hybrid_mla_diffattn_gated_block"


def rmsnorm(x, g):
    xf = x.astype(jnp.float32)
    y = xf * lax.rsqrt(jnp.mean(xf * xf, axis=-1, keepdims=True) + NORM_EPS)
    return (y * g.astype(jnp.float32)).astype(x.dtype)


def apply_rope(x, positions, rot_dim):
    half = rot_dim // 2
    inv_freq = jnp.exp(-math.log(ROPE_THETA) * jnp.arange(half, dtype=jnp.float32) * (2.0 / rot_dim))
    ang = positions.astype(jnp.float32)[..., None] * inv_freq
    ang = ang.reshape(ang.shape[:2] + (1,) * (x.ndim - 3) + (half,))
    cos, sin = jnp.cos(ang), jnp.sin(ang)
    xr = x[..., :rot_dim].astype(jnp.float32)
    x1, x2 = xr[..., :half], xr[..., half:]
    rot = jnp.concatenate([x1 * cos - x2 * sin, x2 * cos + x1 * sin], axis=-1).astype(x.dtype)
    return jnp.concatenate([rot, x[..., rot_dim:]], axis=-1)


def causal_attention(q, k, v):
    B, S, H, Dk = q.shape
    Dv = v.shape[-1]
    nb = S // BLOCK_Q
    scale = Dk ** -0.5
    qb = q.reshape(B, nb, BLOCK_Q, H, Dk).transpose(1, 0, 2, 3, 4)
    kpos = jnp.arange(S)

    def one_block(args):
        qi, i = args
        s = jnp.einsum('bqhd,bkhd->bhqk', qi, k).astype(jnp.float32) * scale
        qpos = i * BLOCK_Q + jnp.arange(BLOCK_Q)
        s = jnp.where(kpos[None, :] <= qpos[:, None], s, -jnp.inf)
        p = jax.nn.softmax(s, axis=-1).astype(v.dtype)
        return jnp.einsum('bhqk,bkhd->bqhd', p, v)

    out = lax.map(one_block, (qb, jnp.arange(nb)))
    return out.transpose(1, 0, 2, 3, 4).reshape(B, S, H, Dv)


def setup_inputs(seed: int = 0) -> dict:
    key = jax.random.key(seed)
    ks = jax.random.split(key, 24)

    def dense(k, fan_in, fan_out):
        return jax.random.normal(k, (DEPTH, fan_in, fan_out), jnp.float32) * fan_in ** -0.5

    def gain(k, n):
        return 1.0 + 0.02 * jax.random.normal(k, (DEPTH, n), jnp.float32)

    def small(k, n, s):
        return s * jax.random.normal(k, (DEPTH, n), jnp.float32)

    x = jax.random.normal(ks[0], (BATCH, SEQ, D_MODEL), jnp.float32)
    positions = jnp.broadcast_to(jnp.arange(SEQ, dtype=jnp.int32)[None, :], (BATCH, SEQ))
    return {
        "x": x,
        "positions": positions,
        "norm_mix_g": gain(ks[1], D_MODEL),
        "w_in": dense(ks[2], D_MODEL, IN_WIDTH),
        "b_gate": small(ks[3], N_BRANCHES * D_MODEL, 0.1),
        "mla_q_norm_g": gain(ks[4], MLA_Q_RANK),
        "mla_w_uq": dense(ks[5], MLA_Q_RANK, MLA_HEADS * MLA_QK_DIM),
        "mla_kv_norm_g": gain(ks[6], MLA_KV_RANK),
        "mla_w_ukv": dense(ks[7], MLA_KV_RANK, MLA_HEADS * (MLA_NOPE_DIM + MLA_V_DIM)),
        "diff_lambda_q1": small(ks[8], DIFF_HEAD_DIM, 0.1),
        "diff_lambda_k1": small(ks[9], DIFF_HEAD_DIM, 0.1),
        "diff_lambda_q2": small(ks[10], DIFF_HEAD_DIM, 0.1),
        "diff_lambda_k2": small(ks[11], DIFF_HEAD_DIM, 0.1),
        "diff_subln_g": gain(ks[12], DIFF_V_DIM),
        "w_branch_mla": dense(ks[13], MLA_WIDTH, D_MODEL),
        "w_branch_diff": dense(ks[14], DIFF_WIDTH, D_MODEL),
        "w_out": dense(ks[15], D_MODEL, D_MODEL),
        "norm_ffn_g": gain(ks[16], D_MODEL),
        "w_ffn_gate": dense(ks[17], D_MODEL, FFN_HIDDEN),
        "w_ffn_up": dense(ks[18], D_MODEL, FFN_HIDDEN),
        "w_ffn_down": dense(ks[19], FFN_HIDDEN, D_MODEL),
        "norm_final_g": 1.0 + 0.02 * jax.random.normal(ks[20], (D_MODEL,), jnp.float32),
    }


def reference(x, positions, norm_mix_g, w_in, b_gate, mla_q_norm_g, mla_w_uq, mla_kv_norm_g, mla_w_ukv,
              diff_lambda_q1, diff_lambda_k1, diff_lambda_q2, diff_lambda_k2, diff_subln_g,
              w_branch_mla, w_branch_diff, w_out, norm_ffn_g, w_ffn_gate, w_ffn_up, w_ffn_down, norm_final_g):
    B, S, _ = x.shape
    split_idx = [int(v) for v in np.cumsum(IN_SIZES)[:-1]]
    for l in range(DEPTH):
        xn = rmsnorm(x, norm_mix_g[l])
        proj = jnp.einsum('bsd,de->bse', xn, w_in[l])
        q_lat, kv_lat, k_rope, dq, dk, dv, gate_pre = jnp.split(proj, split_idx, axis=-1)

        q = jnp.einsum('bsr,re->bse', rmsnorm(q_lat, mla_q_norm_g[l]), mla_w_uq[l])
        q = q.reshape(B, S, MLA_HEADS, MLA_QK_DIM)
        q_nope, q_pe = q[..., :MLA_NOPE_DIM], apply_rope(q[..., MLA_NOPE_DIM:], positions, MLA_ROPE_DIM)
        kv = jnp.einsum('bsr,re->bse', rmsnorm(kv_lat, mla_kv_norm_g[l]), mla_w_ukv[l])
        kv = kv.reshape(B, S, MLA_HEADS, MLA_NOPE_DIM + MLA_V_DIM)
        k_nope, v_mla = kv[..., :MLA_NOPE_DIM], kv[..., MLA_NOPE_DIM:]
        k_pe = apply_rope(k_rope[:, :, None, :], positions, MLA_ROPE_DIM)
        q_mla = jnp.concatenate([q_nope, q_pe], axis=-1)
        k_mla = jnp.concatenate([k_nope, jnp.broadcast_to(k_pe, (B, S, MLA_HEADS, MLA_ROPE_DIM))], axis=-1)
        o_mla = causal_attention(q_mla, k_mla, v_mla).reshape(B, S, MLA_WIDTH)

        dq = apply_rope(dq.reshape(B, S, DIFF_HEADS, 2, DIFF_HEAD_DIM), positions, DIFF_ROT_DIM)
        dk = apply_rope(dk.reshape(B, S, DIFF_HEADS, 2, DIFF_HEAD_DIM), positions, DIFF_ROT_DIM)
        dv = dv.reshape(B, S, DIFF_HEADS, DIFF_V_DIM)
        o1 = causal_attention(dq[..., 0, :], dk[..., 0, :], dv)
        o2 = causal_attention(dq[..., 1, :], dk[..., 1, :], dv)
        lambda_init = 0.8 - 0.6 * math.exp(-0.3 * l)
        lam = (jnp.exp(jnp.sum(diff_lambda_q1[l].astype(jnp.float32) * diff_lambda_k1[l].astype(jnp.float32)))
               - jnp.exp(jnp.sum(diff_lambda_q2[l].astype(jnp.float32) * diff_lambda_k2[l].astype(jnp.float32)))
               + lambda_init).astype(o1.dtype)
        o_diff = rmsnorm(o1 - lam * o2, diff_subln_g[l]) * (1.0 - lambda_init)
        o_diff = o_diff.reshape(B, S, DIFF_WIDTH)

        gates = jax.nn.sigmoid(gate_pre + b_gate[l])
        g_mla, g_diff = gates[..., :D_MODEL], gates[..., D_MODEL:]
        merged = (g_mla * jnp.einsum('bse,ed->bsd', o_mla, w_branch_mla[l])
                  + g_diff * jnp.einsum('bse,ed->bsd', o_diff, w_branch_diff[l]))
        x = x + jnp.einsum('bsd,de->bse', merged, w_out[l])

        hn = rmsnorm(x, norm_ffn_g[l])
        hid = jax.nn.silu(jnp.einsum('bsd,df->bsf', hn, w_ffn_gate[l])) * jnp.einsum('bsd,df->bsf', hn, w_ffn_up[l])
        x = x + jnp.einsum('bsf,fd->bsd', hid, w_ffn_down[l])
    return rmsnorm(x, norm_final_g)
```

```python
import math
import contextlib
import numpy as np
import concourse.bass as bass
import concourse.mybir as mybir
from concourse.bass_utils import run_bass_kernel_spmd

F32 = mybir.dt.float32
BF16 = mybir.dt.bfloat16
I32 = mybir.dt.int32
AF = mybir.ActivationFunctionType
ALU = mybir.AluOpType

D = 1024
EPS = 1e-6
THETA = 500000.0
INV2PI = 1.0 / (2.0 * math.pi)
C1 = 6.28125
C2 = 2.0 * math.pi - 6.28125
NEG = -30000.0
PI_LO = 3.14159
CG_MIX, CB_GATE, CG_Q, CG_KV, CG_SUB, CG_FFN, CG_FIN = 0, 8, 24, 27, 29, 30, 38
C_IFD, C_SGD, C_IFM, C_SGM = 46, 47, 48, 49
C_LQ1, C_LK1, C_LQ2, C_LK2 = 50, 114, 178, 242
NCST = 306
WSTAGE = 1024


class _E:
    def __init__(self, name, eng, sem, step):
        self.name, self.eng, self.sem, self.step, self.n = name, eng, sem, step, 0
        self.waited = {}


class _B:
    def __init__(self):
        self.w = None
        self.r = {}


def build(S):
    To = S // 4
    G = S // 512
    Go = To // 512
    NT = S // 128
    nc = bass.Bass("TRN2", target_bir_lowering=False)

    def din(name, shape, dt=F32):
        return nc.dram_tensor(name, list(shape), dt, kind="ExternalInput").ap()

    x_full = din("x_full", [S, D])
    x_own = din("x_own", [To, D])
    posb_full = din("posb_full", [128, S], I32)
    posb_own = din("posb_own", [128, To], I32)
    cst_d = din("cst", [128, NCST])
    ident_d = din("ident", [128, 128])
    maskT_d = din("maskT", [128, 128])
    ipad_d = din("ipad", [128, 32])
    w_dk_d = din("w_dk", [2, 128, 8 * 256])
    w_dkP_d = din("w_dkP", [2, 128, 8 * 256])
    w_dq_d = din("w_dq", [2, 128, 8 * 256])
    w_dqP_d = din("w_dqP", [2, 128, 8 * 256])
    w_dv_d = din("w_dv", [2, 128, 8 * 256])
    w_kvl_d = din("w_kvl", [128, 8 * 256])
    w_kr_d = din("w_kr", [128, 8 * 96])
    w_krP_d = din("w_krP", [128, 8 * 96])
    w_ql_d = din("w_ql", [128, 8 * 384])
    w_ukv_d = din("w_ukv", [128, 2 * 1024])
    w_uq_d = din("w_uq", [128, 3 * 768])
    w_uqP_d = din("w_uqP", [128, 3 * 768])
    w_g_d = din("w_g", [16, 128, 8 * 128])
    w_bm_d = din("w_bm", [8, 128, 4 * 128])
    w_bd_d = din("w_bd", [8, 128, 4 * 128])
    w_o_d = din("w_o", [8, 128, 8 * 128])
    w_fg_d = din("w_fg", [22, 128, 8 * 128])
    w_fu_d = din("w_fu", [22, 128, 8 * 128])
    w_fd_d = din("w_fd", [8, 128, 22 * 128])
    y_d = nc.dram_tensor("y", [To, D], F32, kind="ExternalOutput").ap()

    es = contextlib.ExitStack()
    with es:
        def sem(name):
            return es.enter_context(nc.semaphore(name))

        PE = _E("pe", nc.tensor, sem("s_pe"), 1)
        ACT = _E("act", nc.scalar, sem("s_act"), 1)
        DVE = _E("dve", nc.vector, sem("s_dve"), 1)
        POOL = _E("pool", nc.gpsimd, sem("s_pool"), 1)
        SP = _E("sp", nc.sync, sem("s_sp"), 1)
        NDQ = 40
        DQ = [_E("dq%d" % i, None, sem("s_dq%d" % i), 16) for i in range(NDQ)]
        dq_ctr = [0]

        def _deps(E, reads, writes, pe_acc=False):
            deps = {}
            def add(t):
                if t is None:
                    return
                e2, v = t
                if deps.get(e2.name, (None, 0))[1] < v:
                    deps[e2.name] = (e2, v)
            for b in reads:
                add(b.w)
            for b in writes:
                add(b.w)
                for t in b.r.values():
                    add(t)
            for name, (e2, v) in deps.items():
                if e2 is E and E is PE:
                    continue
                if E.waited.get(name, 0) < v:
                    E.eng.wait_ge(e2.sem, v)
                    E.waited[name] = v

        def _mark(tk, reads, writes):
            for b in reads:
                e2, v = tk
                if b.r.get(e2.name, (None, 0))[1] < v:
                    b.r[e2.name] = tk
            for b in writes:
                b.w = tk
                b.r = {}

        def op(E, fn, reads=(), writes=()):
            _deps(E, reads, writes)
            ins = fn()
            ins.then_inc(E.sem, 1)
            E.n += 1
            _mark((E, E.n), reads, writes)

        def dma(fn, reads=(), writes=()):
            _deps(SP, reads, writes)
            q = DQ[dq_ctr[0] % NDQ]
            dq_ctr[0] += 1
            if q.n > 0 and SP.waited.get(q.name, 0) < q.n:
                nc.sync.wait_ge(q.sem, q.n)
                SP.waited[q.name] = q.n
            ins = fn()
            ins.then_inc(q.sem, 16)
            q.n += 16
            _mark((q, q.n), reads, writes)
            return (q, q.n)

        def barrier():
            srcs = [PE, ACT, DVE, POOL] + [q for q in DQ if q.n > 0]
            for E in (PE, ACT, DVE, POOL, SP):
                for e2 in srcs:
                    if e2.n > 0 and E.waited.get(e2.name, 0) < e2.n:
                        E.eng.wait_ge(e2.sem, e2.n)
                        E.waited[e2.name] = e2.n

        class T:
            def __init__(self, name, shape, dt, psum=False):
                uid[0] += 1
                name = "t%d_%s" % (uid[0], name)
                if psum:
                    self.t = es_cur[0].enter_context(nc.psum_tensor(name, list(shape), dt))
                else:
                    self.t = es_cur[0].enter_context(nc.sbuf_tensor(name, list(shape), dt))
                self.b = _B()

        es_cur = [es]
        uid = [0]

        cst = T("cst", [128, NCST], F32)
        cc = T("cc", [128, 4], F32)
        ident_f = T("ident_f", [128, 128], F32)
        ident_b = T("ident_b", [128, 128], BF16)
        ones_f = T("ones_f", [128, 128], F32)
        ones_b = T("ones_b", [128, 128], BF16)
        maskT_b = T("maskT_b", [128, 128], BF16)
        ipad_b = T("ipad_b", [128, 32], BF16)
        lam = T("lam", [128, 4], F32)
        ctmp = T("ctmp", [128, 128], F32)
        junk = es.enter_context(nc.sbuf_tensor("t_junk", [128, 1024], BF16))
        wst = [T("wst%d" % i, [128, WSTAGE], F32) for i in range(2)]
        wst_ctr = [0]

        dma(lambda: nc.sync.dma_start(out=cst.t[:], in_=cst_d[:]), writes=[cst.b])
        dma(lambda: nc.sync.dma_start(out=ident_f.t[:], in_=ident_d[:]), writes=[ident_f.b])
        op(DVE, lambda: nc.vector.memset(cc.t[:, 0:1], EPS), writes=[cc.b])
        op(DVE, lambda: nc.vector.memset(cc.t[:, 1:2], math.pi / 2.0), writes=[cc.b])
        op(DVE, lambda: nc.vector.memset(ones_f.t[:], 1.0), writes=[ones_f.b])
        op(DVE, lambda: nc.vector.memset(ones_b.t[:], 1.0), writes=[ones_b.b])
        op(DVE, lambda: nc.vector.tensor_copy(out=ident_b.t[:], in_=ident_f.t[:]), reads=[ident_f.b], writes=[ident_b.b])
        dma(lambda: nc.sync.dma_start(out=ctmp.t[:], in_=maskT_d[:]), writes=[ctmp.b])
        op(DVE, lambda: nc.vector.tensor_copy(out=maskT_b.t[:], in_=ctmp.t[:]), reads=[ctmp.b], writes=[maskT_b.b])
        dma(lambda: nc.sync.dma_start(out=ctmp.t[:, 0:32], in_=ipad_d[:]), writes=[ctmp.b])
        op(DVE, lambda: nc.vector.tensor_copy(out=ipad_b.t[:], in_=ctmp.t[:, 0:32]), reads=[ctmp.b], writes=[ipad_b.b])
        for i, (a, b_) in enumerate(((C_LQ1, C_LK1), (C_LQ2, C_LK2))):
            op(DVE, lambda a=a, b_=b_: nc.vector.tensor_tensor(out=ctmp.t[:, 0:64], in0=cst.t[:, a:a + 64], in1=cst.t[:, b_:b_ + 64], op=ALU.mult),
               reads=[cst.b], writes=[ctmp.b])
            op(DVE, lambda i=i: nc.vector.reduce_sum(out=lam.t[:, i:i + 1], in_=ctmp.t[:, 0:64], axis=mybir.AxisListType.X),
               reads=[ctmp.b], writes=[lam.b])
        op(ACT, lambda: nc.scalar.activation(out=lam.t[:, 0:2], in_=lam.t[:, 0:2], func=AF.Exp), reads=[lam.b], writes=[lam.b])
        op(DVE, lambda: nc.vector.tensor_tensor(out=lam.t[:, 2:3], in0=lam.t[:, 1:2], in1=lam.t[:, 0:1], op=ALU.subtract), reads=[lam.b], writes=[lam.b])
        op(DVE, lambda: nc.vector.tensor_scalar(out=lam.t[:, 2:3], in0=lam.t[:, 2:3], scalar1=1.0, scalar2=-0.2, op0=ALU.mult, op1=ALU.add), reads=[lam.b], writes=[lam.b])
        op(DVE, lambda: nc.vector.tensor_scalar(out=lam.t[:, 3:4], in0=cst.t[:, CG_SUB:CG_SUB + 1], scalar1=0.8, scalar2=0.0, op0=ALU.mult, op1=ALU.add), reads=[cst.b, lam.b], writes=[lam.b])

        cast_mode = ["pool"]
        cast_ctr = [0]

        def load_w(dst, dst_ap, src_ap, n):
            off = 0
            while off < n:
                m = min(WSTAGE, n - off)
                st = wst[wst_ctr[0] % len(wst)]
                wst_ctr[0] += 1
                dma(lambda st=st, off=off, m=m: nc.sync.dma_start(out=st.t[:, 0:m], in_=src_ap[:, off:off + m]), writes=[st.b])
                if cast_mode[0] == "pool":
                    op(POOL, lambda st=st, off=off, m=m: nc.gpsimd.tensor_copy(out=dst_ap[:, off:off + m], in_=st.t[:, 0:m]), reads=[st.b], writes=[dst.b])
                else:
                    k = cast_ctr[0] % 2
                    cast_ctr[0] += 1
                    if k == 0:
                        op(ACT, lambda st=st, off=off, m=m: nc.scalar.copy(out=dst_ap[:, off:off + m], in_=st.t[:, 0:m]), reads=[st.b], writes=[dst.b])
                    else:
                        op(DVE, lambda st=st, off=off, m=m: nc.vector.tensor_copy(out=dst_ap[:, off:off + m], in_=st.t[:, 0:m]), reads=[st.b], writes=[dst.b])
                off += m

        o_mla = T("o_mla", [128, 4, To], BF16)
        o_diff = T("o_diff", [128, 4, To], BF16)
        Osb = T("Osb", [128, 512], F32)
        sumsb = T("sumsb", [1, 512], F32)
        o1n = T("o1n", [128, 512], F32)
        dtmp = T("dtmp", [128, 512], F32)
        dtmp2 = T("dtmp2", [128, 512], F32)
        Pb = [T("Pb%d" % i, [128, 512], BF16) for i in range(4)]
        accP = T("accP", [128, 512], F32)
        Ctab = T("Ctab", [128, 512], F32)
        Stab = T("Stab", [128, 512], F32)
        rt1 = dtmp
        rt2 = dtmp2
        recb = dtmp2
        tile_ctr = [0]

        class XPc:
            pass
        XP = XPc()

        def alloc_xp(tables_only=False, nx=3):
            XP.nx = nx
            if not tables_only:
                XP.xin = [T("xin%d" % i, [128, D], F32) for i in range(nx)]
                XP.xs_bf = [T("xs_bf%d" % i, [128, D], BF16) for i in range(2)]
                XP.xnT = [T("xnT%d" % i, [128, 8, 512], BF16) for i in range(2)]
                XP.ssq = [T("ssq%d" % i, [128, 2], F32) for i in range(nx)]
            XP.posi = T("posi", [128, 512], I32)
            XP.tb_a = T("tb_a", [128, 512], F32)
            XP.tb_t = T("tb_t", [128, 512], F32)
            XP.tb_k = T("tb_k", [128, 512], I32)
            XP.tb_kf = T("tb_kf", [128, 512], F32)
            XP.tb_kf2 = T("tb_kf2", [128, 512], F32)

        def make_tables(pos_src, col0, c_if, c_sg, p0, p1, Cout, Sout, ccol0=0):
            posi, tb_a, tb_t, tb_k, tb_kf = XP.posi, XP.tb_a, XP.tb_t, XP.tb_k, XP.tb_kf
            posf = tb_t
            tb_r = tb_kf
            dma(lambda: nc.sync.dma_start(out=posi.t[p0:p1, :], in_=pos_src[p0:p1, col0:col0 + 512]), writes=[posi.b])
            op(DVE, lambda: nc.vector.tensor_copy(out=posf.t[p0:p1, :], in_=posi.t[p0:p1, :]), reads=[posi.b], writes=[posf.b])
            op(DVE, lambda: nc.vector.tensor_scalar_mul(tb_a.t[p0:p1, :], posf.t[p0:p1, :], cst.t[p0:p1, c_if:c_if + 1]),
               reads=[posf.b, cst.b], writes=[tb_a.b])
            for which in (0, 1):
                tb_kf = XP.tb_kf if which == 0 else XP.tb_kf2
                tb_r = tb_kf
                sh = 0.0 if which == 0 else 0.25
                op(DVE, lambda sh=sh: nc.vector.tensor_scalar(out=tb_t.t[p0:p1, :], in0=tb_a.t[p0:p1, :], scalar1=INV2PI, scalar2=sh, op0=ALU.mult, op1=ALU.add),
                   reads=[tb_a.b], writes=[tb_t.b])
                op(DVE, lambda: nc.vector.tensor_copy(out=tb_k.t[p0:p1, :], in_=tb_t.t[p0:p1, :]), reads=[tb_t.b], writes=[tb_k.b])
                op(DVE, lambda: nc.vector.tensor_copy(out=tb_kf.t[p0:p1, :], in_=tb_k.t[p0:p1, :]), reads=[tb_k.b], writes=[tb_kf.b])
                op(DVE, lambda: nc.vector.scalar_tensor_tensor(out=tb_t.t[p0:p1, :], in0=tb_kf.t[p0:p1, :], scalar=-C1, in1=tb_a.t[p0:p1, :], op0=ALU.mult, op1=ALU.add),
                   reads=[tb_kf.b, tb_a.b], writes=[tb_t.b])
                op(DVE, lambda: nc.vector.scalar_tensor_tensor(out=tb_r.t[p0:p1, :], in0=tb_kf.t[p0:p1, :], scalar=-C2, in1=tb_t.t[p0:p1, :], op0=ALU.mult, op1=ALU.add),
                   reads=[tb_kf.b, tb_t.b], writes=[tb_r.b])
                lo = -PI_LO if which == 0 else -PI_LO - math.pi / 2.0
                hi = PI_LO if which == 0 else PI_LO - math.pi / 2.0
                op(DVE, lambda lo=lo, hi=hi: nc.vector.tensor_scalar(out=tb_r.t[p0:p1, :], in0=tb_r.t[p0:p1, :], scalar1=hi, scalar2=lo, op0=ALU.min, op1=ALU.max),
                   reads=[tb_r.b], writes=[tb_r.b])
                if which == 0:
                    op(ACT, lambda: nc.scalar.activation(out=Sout.t[p0:p1, ccol0:ccol0 + 512], in_=tb_r.t[p0:p1, :], func=AF.Sin, scale=cst.t[p0:p1, c_sg:c_sg + 1]),
                       reads=[tb_r.b, cst.b], writes=[Sout.b])
                else:
                    op(ACT, lambda: nc.scalar.activation(out=Cout.t[p0:p1, ccol0:ccol0 + 512], in_=tb_r.t[p0:p1, :], func=AF.Sin, bias=cc.t[p0:p1, 1:2]),
                       reads=[tb_r.b, cc.b], writes=[Cout.b])

        def xpass_group(xsrc, g, pt, gi, per_tile=None):
            xo = XP.xnT[gi % 2]
            xin, ssq, xs_bf = XP.xin, XP.ssq, XP.xs_bf
            for j in range(4):
                k = tile_ctr[0]
                tile_ctr[0] += 1
                xi = xin[k % XP.nx]
                sq = ssq[k % XP.nx]
                xs = xs_bf[k % 2]
                r0 = g * 512 + j * 128
                dma(lambda xi=xi, r0=r0: nc.sync.dma_start(out=xi.t[:], in_=xsrc[r0:r0 + 128, :]), writes=[xi.b])
                op(ACT, lambda xi=xi, sq=sq: nc.scalar.activation(out=junk[:, :], in_=xi.t[:], func=AF.Square, accum_out=sq.t[:, 0:1]),
                   reads=[xi.b], writes=[sq.b])
                op(ACT, lambda sq=sq: nc.scalar.activation(out=sq.t[:, 1:2], in_=sq.t[:, 0:1], func=AF.Sqrt, scale=1.0 / D, bias=cc.t[:, 0:1]),
                   reads=[sq.b, cc.b], writes=[sq.b])
                op(DVE, lambda sq=sq: nc.vector.reciprocal(out=sq.t[:, 1:2], in_=sq.t[:, 1:2]), reads=[sq.b], writes=[sq.b])
                op(DVE, lambda xi=xi, sq=sq, xs=xs: nc.vector.tensor_scalar_mul(xs.t[:], xi.t[:], sq.t[:, 1:2]),
                   reads=[xi.b, sq.b], writes=[xs.b])
                for c in range(8):
                    p = pt[c // 4]
                    op(PE, lambda c=c, j=j, p=p, xs=xs: nc.tensor.transpose(out=p.t[:, c % 4, j * 128:(j + 1) * 128], in_=xs.t[:, c * 128:(c + 1) * 128], identity=ident_b.t[:]),
                       reads=[xs.b, ident_b.b], writes=[p.b])
                if per_tile is not None:
                    per_tile(j, xi)
            for c in range(8):
                p = pt[c // 4]
                if c % 2 == 0:
                    op(ACT, lambda c=c, p=p: nc.scalar.mul(out=xo.t[:, c, :], in_=p.t[:, c % 4, :], mul=cst.t[:, CG_MIX + c:CG_MIX + c + 1]),
                       reads=[p.b, cst.b], writes=[xo.b])
                else:
                    op(DVE, lambda c=c, p=p: nc.vector.tensor_scalar_mul(xo.t[:, c, :], p.t[:, c % 4, :], cst.t[:, CG_MIX + c:CG_MIX + c + 1]),
                       reads=[p.b, cst.b], writes=[xo.b])
            return xo

        gi_glob = [0]

        def pipelined(xsrc, n, pt, tables_fn, stage2_fn):
            xs = [None] * n
            xs[0] = xpass_group(xsrc, 0, pt, gi_glob[0])
            gi_glob[0] += 1
            for g in range(n):
                if g + 1 < n:
                    xs[g + 1] = xpass_group(xsrc, g + 1, pt, gi_glob[0])
                    gi_glob[0] += 1
                if tables_fn is not None:
                    tables_fn(g)
                stage2_fn(g, xs[g])

        def proj_fm(ps, xo, w, wcols, col0, m, nk=8):
            for c in range(nk):
                op(PE, lambda c=c: nc.tensor.matmul(ps.t[0:m, :], lhsT=w.t[:, c, col0:col0 + m], rhs=xo.t[:, c, :], start=(c == 0), stop=(c == nk - 1)),
                   reads=[w.b, xo.b], writes=[ps.b])

        def rope_combine(psA, psB, Ct, St, p0, p1, out_t, out_ap, tcol0=0, out_aps=None):
            op(DVE, lambda: nc.vector.tensor_tensor(out=rt1.t[p0:p1, :], in0=psA.t[p0:p1, :], in1=Ct.t[p0:p1, tcol0:tcol0 + 512], op=ALU.mult),
               reads=[psA.b, Ct.b], writes=[rt1.b])
            op(DVE, lambda: nc.vector.tensor_tensor(out=rt2.t[p0:p1, :], in0=psB.t[p0:p1, :], in1=St.t[p0:p1, tcol0:tcol0 + 512], op=ALU.mult),
               reads=[psB.b, St.b], writes=[rt2.b])
            if out_aps is None:
                out_aps = [(p0, p1, out_ap)]
            for (a0, a1, oap) in out_aps:
                op(POOL, lambda a0=a0, a1=a1, oap=oap: nc.gpsimd.tensor_tensor(out=oap, in0=rt1.t[a0:a1, :], in1=rt2.t[a0:a1, :], op=ALU.add),
                   reads=[rt1.b, rt2.b], writes=[out_t.b])

        def attention(KT, kt_ap, QT, q_ap, Vt, v_ap, dkrows, dvp, sumsep, s, ps_S, ps_O, ps_sum, scale, mb=0, ps_bc=None, mpad=None):
            units = []
            for kt in range(16 * s + 16):
                t = kt - 16 * s
                units.append((kt, 32 * t if t > 0 else 0, t >= 0))
            n = len(units)

            def QK(u):
                kt, c0, diag = units[u]
                S_ = ps_S[u % len(ps_S)]
                op(PE, lambda: nc.tensor.matmul(S_.t[:, c0:512], lhsT=kt_ap(kt), rhs=q_ap(c0), start=True, stop=(not diag)),
                   reads=[KT.b, QT.b], writes=[S_.b])
                if diag:
                    op(PE, lambda: nc.tensor.matmul(S_.t[:, c0:c0 + 32], lhsT=maskT_b.t[mb:mb + dkrows, :], rhs=ipad_b.t[mb:mb + dkrows, :], start=False, stop=True),
                       reads=[maskT_b.b, ipad_b.b], writes=[S_.b])

            def EXP(u):
                kt, c0, diag = units[u]
                S_ = ps_S[u % len(ps_S)]
                P_ = Pb[u % 4]
                op(ACT, lambda: nc.scalar.activation(out=P_.t[:, c0:512], in_=S_.t[:, c0:512], func=AF.Exp, scale=scale),
                   reads=[S_.b], writes=[P_.b])

            def PV(u):
                kt, c0, diag = units[u]
                P_ = Pb[u % 4]
                mo = mpad if mpad is not None else dvp
                op(PE, lambda: nc.tensor.matmul(ps_O.t[0:mo, c0:512], lhsT=v_ap(kt), rhs=P_.t[:, c0:512], start=(u == 0), stop=(u == n - 1)),
                   reads=[Vt.b, P_.b], writes=[ps_O.b])
                if sumsep:
                    if u % 2 == 0:
                        if u == 0:
                            op(DVE, lambda: nc.vector.tensor_copy(out=accP.t[:, :], in_=P_.t[:, :]), reads=[P_.b], writes=[accP.b])
                        else:
                            op(DVE, lambda: nc.vector.tensor_tensor(out=accP.t[:, c0:512], in0=accP.t[:, c0:512], in1=P_.t[:, c0:512], op=ALU.add),
                               reads=[P_.b, accP.b], writes=[accP.b])
                    else:
                        op(PE, lambda: nc.tensor.matmul(ps_sum.t[:, c0:512], lhsT=ones_b.t[:, :], rhs=P_.t[:, c0:512], start=(u == 1), stop=False),
                           reads=[ones_b.b, P_.b], writes=[ps_sum.b])

            LA = len(ps_S) - 1
            for u in range(min(LA, n)):
                QK(u)
            EXP(0)
            for u in range(n):
                if u + LA < n:
                    QK(u + LA)
                if u + 1 < n:
                    EXP(u + 1)
                PV(u)
            nd = dvp if sumsep else dvp - 1
            op(DVE, lambda: nc.vector.tensor_copy(out=Osb.t[0:nd, :], in_=ps_O.t[0:nd, :]), reads=[ps_O.b], writes=[Osb.b])
            bc = ps_bc if ps_bc is not None else ps_S[0]
            if sumsep:
                op(PE, lambda: nc.tensor.matmul(ps_sum.t[:, :], lhsT=ones_f.t[:, :], rhs=accP.t[:, :], start=False, stop=True),
                   reads=[ones_f.b, accP.b], writes=[ps_sum.b])
                op(DVE, lambda: nc.vector.reciprocal(out=recb.t[:, :], in_=ps_sum.t[:, :]), reads=[ps_sum.b], writes=[recb.b])
                return recb, nd
            op(DVE, lambda: nc.vector.tensor_copy(out=sumsb.t[0:1, :], in_=ps_O.t[nd:nd + 1, :]), reads=[ps_O.b], writes=[sumsb.b])
            op(DVE, lambda: nc.vector.reciprocal(out=sumsb.t[0:1, :], in_=sumsb.t[0:1, :]), reads=[sumsb.b], writes=[sumsb.b])
            op(PE, lambda: nc.tensor.matmul(bc.t[0:nd, :], lhsT=ones_f.t[0:1, 0:nd], rhs=sumsb.t[0:1, :], start=True, stop=True),
               reads=[ones_f.b, sumsb.b], writes=[bc.b])
            return bc, nd

        for hp in range(2):
            ph = contextlib.ExitStack()
            barrier()
            es_cur[0] = ph
            with ph:
                KTd = T("KTd", [128, 2, S], BF16)
                Vd = T("Vd", [128, NT, 256], BF16)
                QTd = T("QTd", [128, 2, 2, To], BF16)
                wA = T("wA", [128, 8, 256], BF16)
                wB = T("wB", [128, 8, 256], BF16)
                wV = T("wV", [128, 8, 256], BF16)
                px = contextlib.ExitStack()
                es_cur[0] = px
                pt = [T("pt%d" % i, [128, 4, 512], BF16, psum=True) for i in range(2)]
                pj = [T("pj%d" % i, [128, 512], F32, psum=True) for i in range(4)]
                alloc_xp(nx=4)
                load_w(wA, wA.t[:].rearrange("p c n -> p (c n)"), w_dq_d[hp], 8 * 256)
                load_w(wB, wB.t[:].rearrange("p c n -> p (c n)"), w_dqP_d[hp], 8 * 256)
                op(POOL, lambda: nc.gpsimd.memset(QTd.t[:].rearrange("p a b n -> p (a b n)"), 0.0), writes=[QTd.b])

                def q_stage2(g, xo):
                    for hh in range(2):
                        proj_fm(pj[0], xo, wA, 256, hh * 128, 128)
                        proj_fm(pj[1], xo, wB, 256, hh * 128, 128)
                        rope_combine(pj[0], pj[1], Ctab, Stab, 0, 128, QTd, None,
                                     out_aps=[(0, 64, QTd.t[0:64, hh, 0, g * 512:(g + 1) * 512]), (64, 128, QTd.t[64:128, hh, 1, g * 512:(g + 1) * 512])])
                pipelined(x_own, Go, pt, lambda g: make_tables(posb_own, g * 512, C_IFD, C_SGD, 0, 128, Ctab, Stab), q_stage2)
                load_w(wA, wA.t[:].rearrange("p c n -> p (c n)"), w_dk_d[hp], 8 * 256)
                load_w(wB, wB.t[:].rearrange("p c n -> p (c n)"), w_dkP_d[hp], 8 * 256)
                load_w(wV, wV.t[:].rearrange("p c n -> p (c n)"), w_dv_d[hp], 8 * 256)

                def k_stage2(g, xo):
                    for hh in range(2):
                        proj_fm(pj[0], xo, wA, 256, hh * 128, 128)
                        proj_fm(pj[1], xo, wB, 256, hh * 128, 128)
                        rope_combine(pj[0], pj[1], Ctab, Stab, 0, 128, KTd, KTd.t[:, hh, g * 512:(g + 1) * 512])
                    for j in range(4):
                        pv = pj[2 + (j % 2)]
                        for c in range(8):
                            op(PE, lambda c=c, j=j, pv=pv: nc.tensor.matmul(pv.t[:, 0:256], lhsT=xo.t[:, c, j * 128:(j + 1) * 128], rhs=wV.t[:, c, :], start=(c == 0), stop=(c == 7)),
                               reads=[xo.b, wV.b], writes=[pv.b])
                        op(ACT, lambda j=j, pv=pv, g=g: nc.scalar.copy(out=Vd.t[:, g * 4 + j, :], in_=pv.t[:, 0:256]), reads=[pv.b], writes=[Vd.b])
                pipelined(x_full, G, pt, lambda g: make_tables(posb_full, g * 512, C_IFD, C_SGD, 0, 128, Ctab, Stab), k_stage2)
                barrier()
                px.close()
                py = contextlib.ExitStack()
                es_cur[0] = py
                ps_S = [T("dS%d" % i, [128, 512], F32, psum=True) for i in range(3)]
                ps_Od = [T("dO%d" % i, [128, 512], F32, psum=True) for i in range(2)]
                ps_sumd = T("dsum", [128, 512], F32, psum=True)
                psL = T("dL", [128, 512], F32, psum=True)
                dseq = [0]
                for hh in range(2):
                    h = hp * 2 + hh
                    for s in range(Go):
                        for comp in range(2):
                            b0 = comp * 64
                            ps_O_ = ps_Od[dseq[0] % 2]
                            dseq[0] += 1
                            bc, nd = attention(
                                KTd, lambda kt: KTd.t[:, hh, kt * 128:(kt + 1) * 128],
                                QTd, lambda c0: QTd.t[:, hh, comp, s * 512 + c0:(s + 1) * 512],
                                Vd, lambda kt: Vd.t[:, kt, hh * 128:(hh + 1) * 128],
                                128, 128, True, s, ps_S, ps_O_, ps_sumd, 0.125, mb=0)
                            if comp == 0:
                                op(DVE, lambda: nc.vector.tensor_tensor(out=o1n.t[:], in0=Osb.t[:], in1=bc.t[:], op=ALU.mult),
                                   reads=[Osb.b, bc.b], writes=[o1n.b])
                            else:
                                op(DVE, lambda: nc.vector.tensor_tensor(out=dtmp.t[:], in0=Osb.t[:], in1=bc.t[:], op=ALU.mult),
                                   reads=[Osb.b, bc.b], writes=[dtmp.b])
                                op(DVE, lambda: nc.vector.scalar_tensor_tensor(out=dtmp.t[:], in0=dtmp.t[:], scalar=lam.t[:, 2:3], in1=o1n.t[:], op0=ALU.mult, op1=ALU.add),
                                   reads=[dtmp.b, lam.b, o1n.b], writes=[dtmp.b])
                                op(DVE, lambda: nc.vector.tensor_tensor(out=dtmp2.t[:], in0=dtmp.t[:], in1=dtmp.t[:], op=ALU.mult),
                                   reads=[dtmp.b], writes=[dtmp2.b])
                                op(PE, lambda: nc.tensor.matmul(psL.t[:], lhsT=ones_f.t[:], rhs=dtmp2.t[:], start=True, stop=True),
                                   reads=[ones_f.b, dtmp2.b], writes=[psL.b])
                                op(ACT, lambda: nc.scalar.activation(out=dtmp2.t[:], in_=psL.t[:], func=AF.Sqrt, scale=1.0 / 128, bias=cc.t[:, 0:1]),
                                   reads=[psL.b, cc.b], writes=[dtmp2.b])
                                op(DVE, lambda: nc.vector.reciprocal(out=dtmp2.t[:], in_=dtmp2.t[:]), reads=[dtmp2.b], writes=[dtmp2.b])
                                op(DVE, lambda: nc.vector.scalar_tensor_tensor(out=o_diff.t[:, h, s * 512:(s + 1) * 512], in0=dtmp.t[:], scalar=lam.t[:, 3:4], in1=dtmp2.t[:], op0=ALU.mult, op1=ALU.mult),
                                   reads=[dtmp.b, lam.b, dtmp2.b], writes=[o_diff.b])
                barrier()
                py.close()
                es_cur[0] = ph
            barrier()
        es_cur[0] = es

        ph = contextlib.ExitStack()
        es_cur[0] = ph
        with ph:
            kvnT = T("kvnT", [128, 2, S], BF16)
            KTm = [T("KTm%d" % i, [96, S], BF16) for i in range(2)]
            qnT = T("qnT", [128, 3, To], BF16)
            warena = T("warena", [128, 6656], BF16)

            class _V:
                def __init__(self, off, c, n):
                    self.t = warena.t[:, off:off + c * n].rearrange("p (c n) -> p c n", c=c)
                    self.flat = warena.t[:, off:off + c * n]
                    self.b = warena.b
            w_ql = _V(0, 8, 384)
            w_kvl = _V(0, 8, 256)
            w_kr = _V(2048, 8, 96)
            w_krP = _V(2048 + 768, 8, 96)
            w_ukv = _V(0, 2, 1024)
            w_uq = _V(2048, 3, 768)
            w_uqP = _V(2048 + 2304, 3, 768)

            pa = contextlib.ExitStack()
            es_cur[0] = pa
            with pa:
                alloc_xp()
                lat = T("lat", [128, 3, 512], F32)
                latsq = T("latsq", [128, 512], BF16)
                rsb = Osb
                pt = [T("ptm%d" % i, [128, 4, 512], BF16, psum=True) for i in range(2)]
                pj = [T("pjm%d" % i, [128, 512], F32, psum=True) for i in range(4)]

                def latent_norm(xo, w, nch, gcol, out_t, out_col0):
                    for ch in range(nch):
                        p = pj[ch % 2]
                        proj_fm(p, xo, w, nch * 128, ch * 128, 128)
                        op(ACT, lambda ch=ch, p=p: nc.scalar.copy(out=lat.t[:, ch, :], in_=p.t[:]), reads=[p.b], writes=[lat.b])
                        op(DVE, lambda ch=ch: nc.vector.tensor_tensor(out=latsq.t[:], in0=lat.t[:, ch, :], in1=lat.t[:, ch, :], op=ALU.mult),
                           reads=[lat.b], writes=[latsq.b])
                        op(PE, lambda ch=ch: nc.tensor.matmul(pj[2].t[:], lhsT=ones_b.t[:], rhs=latsq.t[:], start=(ch == 0), stop=(ch == nch - 1)),
                           reads=[ones_b.b, latsq.b], writes=[pj[2].b])
                    op(ACT, lambda: nc.scalar.activation(out=rsb.t[:], in_=pj[2].t[:], func=AF.Sqrt, scale=1.0 / (nch * 128), bias=cc.t[:, 0:1]),
                       reads=[pj[2].b, cc.b], writes=[rsb.b])
                    op(DVE, lambda: nc.vector.reciprocal(out=rsb.t[:], in_=rsb.t[:]), reads=[rsb.b], writes=[rsb.b])
                    for ch in range(nch):
                        op(DVE, lambda ch=ch: nc.vector.scalar_tensor_tensor(out=out_t.t[:, ch, out_col0:out_col0 + 512], in0=lat.t[:, ch, :], scalar=cst.t[:, gcol + ch:gcol + ch + 1], in1=rsb.t[:], op0=ALU.mult, op1=ALU.mult),
                           reads=[lat.b, cst.b, rsb.b], writes=[out_t.b])

                load_w(w_ql, w_ql.flat, w_ql_d, 8 * 384)
                pipelined(x_own, Go, pt, None, lambda g, xo: latent_norm(xo, w_ql, 3, CG_Q, qnT, g * 512))
                load_w(w_kvl, w_kvl.flat, w_kvl_d, 8 * 256)
                load_w(w_kr, w_kr.flat, w_kr_d, 8 * 96)
                load_w(w_krP, w_krP.flat, w_krP_d, 8 * 96)

                def mk_stage2(g, xo):
                    latent_norm(xo, w_kvl, 2, CG_KV, kvnT, g * 512)
                    proj_fm(pj[0], xo, w_kr, 96, 0, 96)
                    proj_fm(pj[1], xo, w_krP, 96, 0, 96)
                    rope_combine(pj[0], pj[1], Ctab, Stab, 64, 96, KTm[0], KTm[0].t[64:96, g * 512:(g + 1) * 512])
                    op(POOL, lambda g=g: nc.gpsimd.tensor_copy(out=KTm[1].t[64:96, g * 512:(g + 1) * 512], in_=KTm[0].t[64:96, g * 512:(g + 1) * 512]),
                       reads=[KTm[0].b], writes=[KTm[1].b])
                pipelined(x_full, G, pt, lambda g: make_tables(posb_full, g * 512, C_IFM, C_SGM, 64, 96, Ctab, Stab), mk_stage2)
            barrier()
            es_cur[0] = ph

            pb = contextlib.ExitStack()
            es_cur[0] = pb
            with pb:
                alloc_xp(tables_only=True)
                Vm = [T("Vm%d" % i, [128, NT * 65 + 64], BF16) for i in range(2)]
                QTm = [T("QTm%d" % i, [96, To], BF16) for i in range(2)]
                CqT = T("CqT", [96, To], F32)
                SqT = T("SqT", [96, To], F32)
                ps_S = [T("psS%d" % i, [128, 512], F32, psum=True) for i in range(3)]
                ps_Os = [T("psO%d" % i, [128, 512], F32, psum=True) for i in range(2)]
                ps_bc = T("psbc", [128, 512], F32, psum=True)
                pbk = [T("pbk%d" % i, [128, 512], F32, psum=True) for i in range(2)]
                seq_ctr = [0]
                load_w(w_ukv, w_ukv.flat, w_ukv_d, 2 * 1024)
                load_w(w_uq, w_uq.flat, w_uq_d, 3 * 768)
                load_w(w_uqP, w_uqP.flat, w_uqP_d, 3 * 768)
                for i in range(2):
                    op(DVE, lambda i=i: nc.vector.memset(Vm[i].t[:, :], 0.0), writes=[Vm[i].b])
                    op(DVE, lambda i=i: nc.vector.memset(Vm[i].t[:, 0:NT * 65].rearrange("p (a b) -> p a b", b=65)[:, :, 64:65], 1.0), writes=[Vm[i].b])
                for g in range(Go):
                    make_tables(posb_own, g * 512, C_IFM, C_SGM, 64, 96, CqT, SqT, ccol0=g * 512)
                bk_ctr = [0]

                def nextbank():
                    p = pbk[bk_ctr[0] % 2]
                    bk_ctr[0] += 1
                    return p

                def build_head(h):
                    KT_, V_, Q_ = KTm[h % 2], Vm[h % 2], QTm[h % 2]
                    for g in range(Go):
                        pA = nextbank()
                        pB = nextbank()
                        for ch in range(3):
                            op(PE, lambda ch=ch, g=g: nc.tensor.matmul(pA.t[0:96, :], lhsT=w_uq.t[:, ch, h * 96:(h + 1) * 96], rhs=qnT.t[:, ch, g * 512:(g + 1) * 512], start=(ch == 0), stop=(ch == 2)),
                               reads=[w_uq.b, qnT.b], writes=[pA.b])
                        for ch in range(3):
                            op(PE, lambda ch=ch, g=g: nc.tensor.matmul(pB.t[0:96, :], lhsT=w_uqP.t[:, ch, h * 96:(h + 1) * 96], rhs=qnT.t[:, ch, g * 512:(g + 1) * 512], start=(ch == 0), stop=(ch == 2)),
                               reads=[w_uqP.b, qnT.b], writes=[pB.b])
                        op(DVE, lambda g=g: nc.vector.tensor_copy(out=Q_.t[0:64, g * 512:(g + 1) * 512], in_=pA.t[0:64, :]), reads=[pA.b], writes=[Q_.b])
                        rope_combine(pA, pB, CqT, SqT, 64, 96, Q_, Q_.t[64:96, g * 512:(g + 1) * 512], tcol0=g * 512)
                    for g in range(G):
                        p = nextbank()
                        for ch in range(2):
                            op(PE, lambda ch=ch, g=g, p=p: nc.tensor.matmul(p.t[0:64, :], lhsT=w_ukv.t[:, ch, h * 128:h * 128 + 64], rhs=kvnT.t[:, ch, g * 512:(g + 1) * 512], start=(ch == 0), stop=(ch == 1)),
                               reads=[w_ukv.b, kvnT.b], writes=[p.b])
                        op(DVE, lambda g=g, p=p: nc.vector.tensor_copy(out=KT_.t[0:64, g * 512:(g + 1) * 512], in_=p.t[0:64, :]), reads=[p.b], writes=[KT_.b])
                    for t8 in range(NT // 8):
                        p = nextbank()
                        for tt in range(8):
                            tok = (t8 * 8 + tt) * 128
                            for ch in range(2):
                                op(PE, lambda ch=ch, tt=tt, tok=tok, p=p: nc.tensor.matmul(p.t[:, tt * 64:(tt + 1) * 64], lhsT=kvnT.t[:, ch, tok:tok + 128], rhs=w_ukv.t[:, ch, h * 128 + 64:h * 128 + 128], start=(ch == 0), stop=(ch == 1)),
                                   reads=[kvnT.b, w_ukv.b], writes=[p.b])
                        op(DVE, lambda t8=t8, p=p: nc.vector.tensor_copy(out=V_.t[:, t8 * 8 * 65:(t8 + 1) * 8 * 65].rearrange("p (a b) -> p a b", b=65)[:, :, 0:64], in_=p.t[:].rearrange("p (a b) -> p a b", a=8)), reads=[p.b], writes=[V_.b])

                build_head(0)
                for h in range(8):
                    if h + 1 < 8:
                        build_head(h + 1)
                    KT_, V_, Q_ = KTm[h % 2], Vm[h % 2], QTm[h % 2]
                    for s in range(Go):
                        ps_O = ps_Os[seq_ctr[0] % 2]
                        seq_ctr[0] += 1
                        bc, nd = attention(
                            KT_, lambda kt: KT_.t[0:96, kt * 128:(kt + 1) * 128],
                            Q_, lambda c0: Q_.t[0:96, s * 512 + c0:(s + 1) * 512],
                            V_, lambda kt: V_.t[:, kt * 65:kt * 65 + 128],
                            96, 65, False, s, ps_S, ps_O, None, 96.0 ** -0.5, ps_bc=ps_bc, mpad=128)
                        po = (h % 2) * 64
                        op(DVE, lambda: nc.vector.tensor_tensor(out=o_mla.t[po:po + 64, h // 2, s * 512:(s + 1) * 512], in0=Osb.t[0:64, :], in1=bc.t[0:64, :], op=ALU.mult),
                           reads=[Osb.b, bc.b], writes=[o_mla.b])
            barrier()
            es_cur[0] = ph
        barrier()
        es_cur[0] = es

        ph = contextlib.ExitStack()
        es_cur[0] = ph
        with ph:
            NSLOT = 5
            slots = [T("wsl%d" % i, [128, 22 * 128], BF16) for i in range(NSLOT)]
            sl_ctr = [0]
            xT = T("xT", [128, 8, 512], F32)
            sig = T("sig", [128, 2, 512], BF16)
            mt1 = dtmp
            mt2 = dtmp2
            merged = T("merged", [128, 8, 512], BF16)
            hn = T("hn", [128, 8, 512], BF16)
            hid = T("hid", [128, 22, 512], BF16)
            sqx = o1n
            rsx = T("rsx", [128, 512], F32)
            sg = Osb
            ysb = T("ysb", [128, D], F32)
            pt = [T("ptt%d" % i, [128, 4, 512], BF16, psum=True) for i in range(2)]
            pj = [T("pjt%d" % i, [128, 512], F32, psum=True) for i in range(4)]

            def wblock(src_ap, n):
                sl = slots[sl_ctr[0] % NSLOT]
                sl_ctr[0] += 1
                load_w(sl, sl.t[:, 0:n], src_ap, n)
                return sl

            def rms_bcast(src):
                for c in range(8):
                    op(DVE, lambda c=c: nc.vector.tensor_tensor(out=sqx.t[:], in0=src.t[:, c, :], in1=src.t[:, c, :], op=ALU.mult),
                       reads=[src.b], writes=[sqx.b])
                    op(PE, lambda c=c: nc.tensor.matmul(pj[3].t[:], lhsT=ones_f.t[:], rhs=sqx.t[:], start=(c == 0), stop=(c == 7)),
                       reads=[ones_f.b, sqx.b], writes=[pj[3].b])
                op(ACT, lambda: nc.scalar.activation(out=rsx.t[:], in_=pj[3].t[:], func=AF.Sqrt, scale=1.0 / D, bias=cc.t[:, 0:1]),
                   reads=[pj[3].b, cc.b], writes=[rsx.b])
                op(DVE, lambda: nc.vector.reciprocal(out=rsx.t[:], in_=rsx.t[:]), reads=[rsx.b], writes=[rsx.b])

            alloc_xp()
            cast_mode[0] = "actdve"
            wst.extend([T("wstx%d" % i, [128, WSTAGE], F32) for i in range(2)])
            jobs = []
            gi_box = [0]

            def add_group(g):
                tsl = slice(g * 512, (g + 1) * 512)
                st = {}

                def pre(_):
                    def xt_tile(j, xi):
                        for half in range(2):
                            p = pj[half]
                            for c4 in range(4):
                                c = half * 4 + c4
                                op(PE, lambda c=c, c4=c4, p=p: nc.tensor.transpose(out=p.t[:, c4 * 128:(c4 + 1) * 128], in_=xi.t[:, c * 128:(c + 1) * 128], identity=ident_f.t[:]),
                                   reads=[xi.b, ident_f.b], writes=[p.b])
                            op(DVE, lambda half=half, p=p: nc.vector.tensor_copy(out=xT.t[:, half * 4:(half + 1) * 4, j * 128:(j + 1) * 128], in_=p.t[:].rearrange("p (a b) -> p a b", a=4)),
                               reads=[p.b], writes=[xT.b])
                    st["xo"] = xpass_group(x_own, g, pt, gi_glob[0], per_tile=xt_tile)
                    gi_glob[0] += 1
                jobs.append((None, 0, pre))

                def gate(br, e):
                    def fn(wg):
                        xo = st["xo"]
                        p = pj[br]
                        for c in range(8):
                            op(PE, lambda c=c: nc.tensor.matmul(p.t[:], lhsT=wg.t[:, c * 128:(c + 1) * 128], rhs=xo.t[:, c, :], start=(c == 0), stop=(c == 7)),
                               reads=[wg.b, xo.b], writes=[p.b])
                        bcol = CB_GATE + br * 8 + e
                        op(ACT, lambda: nc.scalar.activation(out=sig.t[:, br, :], in_=p.t[:], func=AF.Sigmoid, bias=cst.t[:, bcol:bcol + 1]),
                           reads=[p.b, cst.b], writes=[sig.b])
                    return fn

                def bm(e):
                    def fn(wm):
                        for c in range(4):
                            op(PE, lambda c=c: nc.tensor.matmul(pj[2].t[:], lhsT=wm.t[:, c * 128:(c + 1) * 128], rhs=o_mla.t[:, c, tsl], start=(c == 0), stop=(c == 3)),
                               reads=[wm.b, o_mla.b], writes=[pj[2].b])
                    return fn

                def bd(e):
                    def fn(wd):
                        for c in range(4):
                            op(PE, lambda c=c: nc.tensor.matmul(pj[3].t[:], lhsT=wd.t[:, c * 128:(c + 1) * 128], rhs=o_diff.t[:, c, tsl], start=(c == 0), stop=(c == 3)),
                               reads=[wd.b, o_diff.b], writes=[pj[3].b])
                        op(DVE, lambda: nc.vector.tensor_tensor(out=mt1.t[:], in0=pj[2].t[:], in1=sig.t[:, 0, :], op=ALU.mult), reads=[pj[2].b, sig.b], writes=[mt1.b])
                        op(DVE, lambda: nc.vector.tensor_tensor(out=mt2.t[:], in0=pj[3].t[:], in1=sig.t[:, 1, :], op=ALU.mult), reads=[pj[3].b, sig.b], writes=[mt2.b])
                        op(POOL, lambda: nc.gpsimd.tensor_tensor(out=merged.t[:, e, :], in0=mt1.t[:], in1=mt2.t[:], op=ALU.add), reads=[mt1.b, mt2.b], writes=[merged.b])
                    return fn

                for e in range(8):
                    jobs.append((w_g_d[e], 8 * 128, gate(0, e)))
                    jobs.append((w_g_d[8 + e], 8 * 128, gate(1, e)))
                    jobs.append((w_bm_d[e], 4 * 128, bm(e)))
                    jobs.append((w_bd_d[e], 4 * 128, bd(e)))

                def outp(e):
                    def fn(wo):
                        p = pj[e % 2]
                        for c in range(8):
                            op(PE, lambda c=c: nc.tensor.matmul(p.t[:], lhsT=wo.t[:, c * 128:(c + 1) * 128], rhs=merged.t[:, c, :], start=(c == 0), stop=(c == 7)),
                               reads=[wo.b, merged.b], writes=[p.b])
                        op(DVE, lambda: nc.vector.tensor_tensor(out=xT.t[:, e, :], in0=xT.t[:, e, :], in1=p.t[:], op=ALU.add), reads=[xT.b, p.b], writes=[xT.b])
                    return fn
                for e in range(8):
                    jobs.append((w_o_d[e], 8 * 128, outp(e)))

                def ffn_norm(_):
                    rms_bcast(xT)
                    for c in range(8):
                        op(DVE, lambda c=c: nc.vector.scalar_tensor_tensor(out=hn.t[:, c, :], in0=xT.t[:, c, :], scalar=cst.t[:, CG_FFN + c:CG_FFN + c + 1], in1=rsx.t[:], op0=ALU.mult, op1=ALU.mult),
                           reads=[xT.b, cst.b, rsx.b], writes=[hn.b])
                jobs.append((None, 0, ffn_norm))

                def fgate(f):
                    def fn(w):
                        for c in range(8):
                            op(PE, lambda c=c: nc.tensor.matmul(pj[0].t[:], lhsT=w.t[:, c * 128:(c + 1) * 128], rhs=hn.t[:, c, :], start=(c == 0), stop=(c == 7)),
                               reads=[w.b, hn.b], writes=[pj[0].b])
                        op(ACT, lambda: nc.scalar.activation(out=sg.t[:], in_=pj[0].t[:], func=AF.Silu), reads=[pj[0].b], writes=[sg.b])
                    return fn

                def fup(f):
                    def fn(w):
                        for c in range(8):
                            op(PE, lambda c=c: nc.tensor.matmul(pj[1].t[:], lhsT=w.t[:, c * 128:(c + 1) * 128], rhs=hn.t[:, c, :], start=(c == 0), stop=(c == 7)),
                               reads=[w.b, hn.b], writes=[pj[1].b])
                        op(DVE, lambda: nc.vector.tensor_tensor(out=hid.t[:, f, :], in0=sg.t[:], in1=pj[1].t[:], op=ALU.mult), reads=[sg.b, pj[1].b], writes=[hid.b])
                    return fn
                for f in range(22):
                    jobs.append((w_fg_d[f], 8 * 128, fgate(f)))
                    jobs.append((w_fu_d[f], 8 * 128, fup(f)))

                def fdown(e):
                    def fn(w):
                        p = pj[2 + e % 2]
                        for f in range(22):
                            op(PE, lambda f=f: nc.tensor.matmul(p.t[:], lhsT=w.t[:, f * 128:(f + 1) * 128], rhs=hid.t[:, f, :], start=(f == 0), stop=(f == 21)),
                               reads=[w.b, hid.b], writes=[p.b])
                        op(DVE, lambda: nc.vector.tensor_tensor(out=xT.t[:, e, :], in0=xT.t[:, e, :], in1=p.t[:], op=ALU.add), reads=[xT.b, p.b], writes=[xT.b])
                    return fn
                for e in range(8):
                    jobs.append((w_fd_d[e], 22 * 128, fdown(e)))

                def post(_):
                    rms_bcast(xT)
                    for c in range(8):
                        op(DVE, lambda c=c: nc.vector.scalar_tensor_tensor(out=xT.t[:, c, :], in0=xT.t[:, c, :], scalar=cst.t[:, CG_FIN + c:CG_FIN + c + 1], in1=rsx.t[:], op0=ALU.mult, op1=ALU.mult),
                           reads=[xT.b, cst.b, rsx.b], writes=[xT.b])
                    for j in range(4):
                        for half in range(2):
                            p = pj[half]
                            for c4 in range(4):
                                c = half * 4 + c4
                                op(PE, lambda c=c, c4=c4, j=j, p=p: nc.tensor.transpose(out=p.t[:, c4 * 128:(c4 + 1) * 128], in_=xT.t[:, c, j * 128:(j + 1) * 128], identity=ident_f.t[:]),
                                   reads=[xT.b, ident_f.b], writes=[p.b])
                            op(ACT, lambda half=half, p=p: nc.scalar.copy(out=ysb.t[:, half * 512:(half + 1) * 512], in_=p.t[:]), reads=[p.b], writes=[ysb.b])
                        r0 = g * 512 + j * 128
                        dma(lambda r0=r0: nc.sync.dma_start(out=y_d[r0:r0 + 128, :], in_=ysb.t[:]), reads=[ysb.b])
                jobs.append((None, 0, post))

            for g in range(Go):
                add_group(g)
            PD = 3
            loaded = {}
            nj = len(jobs)

            def issue_load(i):
                src, n, fn = jobs[i]
                if src is not None:
                    loaded[i] = wblock(src, n)

            widx = [i for i in range(nj) if jobs[i][0] is not None]
            nxt = [0]

            def prefetch_upto(k):
                while nxt[0] < len(widx) and nxt[0] < k:
                    issue_load(widx[nxt[0]])
                    nxt[0] += 1
            wseen = 0
            for i in range(nj):
                src, n, fn = jobs[i]
                if src is not None:
                    prefetch_upto(wseen + 1 + PD)
                    wseen += 1
                    fn(loaded.pop(i))
                else:
                    fn(None)
            for q in DQ:
                if q.n > 0:
                    nc.sync.wait_ge(q.sem, q.n)
        barrier()
        es_cur[0] = es
    return nc


def _wl(W):
    Din, n = W.shape
    C = Din // 128
    return np.ascontiguousarray(W.reshape(C, 128, n).transpose(1, 0, 2).reshape(128, C * n)).astype(np.float32)


def _partner_cols(W, groups):
    P = np.zeros_like(W)
    for st, half in groups:
        P[:, st:st + half] = W[:, st + half:st + 2 * half]
        P[:, st + half:st + 2 * half] = W[:, st:st + half]
    return P


def _inv_freq(rot_dim):
    half = rot_dim // 2
    return np.exp((-math.log(THETA)) * np.arange(half, dtype=np.float32) * np.float32(2.0 / rot_dim)).astype(np.float32)


def prepare_weights(inp):
    w_in = np.asarray(inp["w_in"], np.float32)[0]
    d = {}
    dq = w_in[:, 672:1184]
    dk = w_in[:, 1184:1696]
    dv = w_in[:, 1696:2208]
    grp = [(i * 64, 8) for i in range(4)]
    d["w_dq"] = np.stack([_wl(dq[:, hp * 256:(hp + 1) * 256]) for hp in range(2)])
    d["w_dqP"] = np.stack([_wl(_partner_cols(dq[:, hp * 256:(hp + 1) * 256], grp)) for hp in range(2)])
    d["w_dk"] = np.stack([_wl(dk[:, hp * 256:(hp + 1) * 256]) for hp in range(2)])
    d["w_dkP"] = np.stack([_wl(_partner_cols(dk[:, hp * 256:(hp + 1) * 256], grp)) for hp in range(2)])
    d["w_dv"] = np.stack([_wl(dv[:, hp * 256:(hp + 1) * 256]) for hp in range(2)])
    d["w_kvl"] = _wl(w_in[:, 384:640])
    kr = np.zeros((D, 96), np.float32)
    kr[:, 64:96] = w_in[:, 640:672]
    d["w_kr"] = _wl(kr)
    d["w_krP"] = _wl(_partner_cols(kr, [(64, 16)]))
    d["w_ql"] = _wl(w_in[:, 0:384])
    d["w_ukv"] = _wl(np.asarray(inp["mla_w_ukv"], np.float32)[0])
    uq = np.asarray(inp["mla_w_uq"], np.float32)[0]
    d["w_uq"] = _wl(uq)
    d["w_uqP"] = _wl(_partner_cols(uq, [(h * 96 + 64, 16) for h in range(8)]))
    wg = w_in[:, 2208:4256]
    d["w_g"] = np.stack([_wl(wg[:, e * 128:(e + 1) * 128]) for e in range(16)])
    bm = np.asarray(inp["w_branch_mla"], np.float32)[0]
    bd = np.asarray(inp["w_branch_diff"], np.float32)[0]
    wo = np.asarray(inp["w_out"], np.float32)[0]
    d["w_bm"] = np.stack([_wl(bm[:, e * 128:(e + 1) * 128]) for e in range(8)])
    d["w_bd"] = np.stack([_wl(bd[:, e * 128:(e + 1) * 128]) for e in range(8)])
    d["w_o"] = np.stack([_wl(wo[:, e * 128:(e + 1) * 128]) for e in range(8)])
    fg = np.asarray(inp["w_ffn_gate"], np.float32)[0]
    fu = np.asarray(inp["w_ffn_up"], np.float32)[0]
    fd = np.asarray(inp["w_ffn_down"], np.float32)[0]
    d["w_fg"] = np.stack([_wl(fg[:, f * 128:(f + 1) * 128]) for f in range(22)])
    d["w_fu"] = np.stack([_wl(fu[:, f * 128:(f + 1) * 128]) for f in range(22)])
    d["w_fd"] = np.stack([_wl(fd[:, e * 128:(e + 1) * 128]) for e in range(8)])
    cst = np.zeros((128, NCST), np.float32)

    def pc(v, C):
        return np.asarray(v, np.float32).reshape(C, 128).T

    cst[:, CG_MIX:CG_MIX + 8] = pc(inp["norm_mix_g"][0], 8)
    cst[:, CB_GATE:CB_GATE + 16] = pc(inp["b_gate"][0], 16)
    cst[:, CG_Q:CG_Q + 3] = pc(inp["mla_q_norm_g"][0], 3)
    cst[:, CG_KV:CG_KV + 2] = pc(inp["mla_kv_norm_g"][0], 2)
    cst[:, CG_SUB:CG_SUB + 1] = pc(inp["diff_subln_g"][0], 1)
    cst[:, CG_FFN:CG_FFN + 8] = pc(inp["norm_ffn_g"][0], 8)
    cst[:, CG_FIN:CG_FIN + 8] = pc(inp["norm_final_g"], 8)
    fdiff = _inv_freq(16)
    fmla = _inv_freq(32)
    for p in range(128):
        j = p % 64
        if j < 8:
            cst[p, C_IFD], cst[p, C_SGD] = fdiff[j], -1.0
        elif j < 16:
            cst[p, C_IFD], cst[p, C_SGD] = fdiff[j - 8], 1.0
        if 64 <= p < 80:
            cst[p, C_IFM], cst[p, C_SGM] = fmla[p - 64], -1.0
        elif 80 <= p < 96:
            cst[p, C_IFM], cst[p, C_SGM] = fmla[p - 80], 1.0
    cst[:, C_LQ1:C_LQ1 + 64] = np.asarray(inp["diff_lambda_q1"], np.float32)[0][None, :]
    cst[:, C_LK1:C_LK1 + 64] = np.asarray(inp["diff_lambda_k1"], np.float32)[0][None, :]
    cst[:, C_LQ2:C_LQ2 + 64] = np.asarray(inp["diff_lambda_q2"], np.float32)[0][None, :]
    cst[:, C_LK2:C_LK2 + 64] = np.asarray(inp["diff_lambda_k2"], np.float32)[0][None, :]
    d["cst"] = cst
    d["ident"] = np.eye(128, dtype=np.float32)
    ip = np.zeros((128, 32), np.float32)
    ip[:32, :32] = np.eye(32, dtype=np.float32)
    ip[64:96, :32] = np.eye(32, dtype=np.float32)
    d["ipad"] = ip
    return d


def _mask_for(r):
    m = np.zeros((128, 128), np.float32)
    kk = np.arange(128)[None, :]
    ii = np.arange(32)[:, None]
    m[:32, :] = np.where(kk <= 4 * ii + r, 0.0, NEG)
    m[64:96, :] = m[:32, :]
    return m


_NC_CACHE = {}


def run(inputs, S):
    x = np.asarray(inputs["x"], np.float32)
    pos = np.asarray(inputs["positions"], np.int32)
    B = x.shape[0]
    wd = prepare_weights(inputs)
    if S not in _NC_CACHE:
        _NC_CACHE[S] = build(S)
    nc = _NC_CACHE[S]
    in_maps = []
    for c in range(8):
        b, r = c // 4, c % 4
        m = dict(wd)
        m["x_full"] = np.ascontiguousarray(x[b])
        m["x_own"] = np.ascontiguousarray(x[b, r::4])
        m["posb_full"] = np.ascontiguousarray(np.broadcast_to(pos[b][None, :], (128, S)))
        m["posb_own"] = np.ascontiguousarray(np.broadcast_to(pos[b, r::4][None, :], (128, S // 4)))
        m["maskT"] = _mask_for(r)
        in_maps.append(m)
    res = run_bass_kernel_spmd(nc, in_maps, core_ids=list(range(8)))
    out = np.zeros((B, S, D), np.float32)
    for c in range(8):
        b, r = c // 4, c % 4
        out[b, r::4] = res.results[c]["y"]
    return out


def kernel(**inputs):
    return run(inputs, 8192)
```

```python
import math
import contextlib
import numpy as np
import concourse.bass as bass
import concourse.mybir as mybir
from concourse.bass_utils import run_bass_kernel_spmd

F32 = mybir.dt.float32
BF16 = mybir.dt.bfloat16
I32 = mybir.dt.int32
AF = mybir.ActivationFunctionType
ALU = mybir.AluOpType

D = 1024
EPS = 1e-6
THETA = 500000.0
INV2PI = 1.0 / (2.0 * math.pi)
C1 = 6.28125
C2 = 2.0 * math.pi - 6.28125
NEG = -30000.0
PI_LO = 3.14159
CG_MIX, CB_GATE, CG_Q, CG_KV, CG_SUB, CG_FFN, CG_FIN = 0, 8, 24, 27, 29, 30, 38
C_IFD, C_SGD, C_IFM, C_SGM = 46, 47, 48, 49
C_LQ1, C_LK1, C_LQ2, C_LK2 = 50, 114, 178, 242
NCST = 306
WSTAGE = 1024


class _E:
    def __init__(self, name, eng, sem, step):
        self.name, self.eng, self.sem, self.step, self.n = name, eng, sem, step, 0
        self.waited = {}


class _B:
    def __init__(self):
        self.w = None
        self.r = {}


def build(S):
    To = S // 4
    G = S // 512
    Go = To // 512
    NT = S // 128
    nc = bass.Bass("TRN2", target_bir_lowering=False)

    def din(name, shape, dt=F32):
        return nc.dram_tensor(name, list(shape), dt, kind="ExternalInput").ap()

    x_full = din("x_full", [S, D])
    x_own = din("x_own", [To, D])
    posb_full = din("posb_full", [128, S], I32)
    posb_own = din("posb_own", [128, To], I32)
    cst_d = din("cst", [128, NCST])
    ident_d = din("ident", [128, 128])
    maskT_d = din("maskT", [128, 128])
    ipad_d = din("ipad", [128, 32])
    w_dk_d = din("w_dk", [2, 128, 8 * 256])
    w_dkP_d = din("w_dkP", [2, 128, 8 * 256])
    w_dq_d = din("w_dq", [2, 128, 8 * 256])
    w_dqP_d = din("w_dqP", [2, 128, 8 * 256])
    w_dv_d = din("w_dv", [2, 128, 8 * 256])
    w_kvl_d = din("w_kvl", [128, 8 * 256])
    w_kr_d = din("w_kr", [128, 8 * 96])
    w_krP_d = din("w_krP", [128, 8 * 96])
    w_ql_d = din("w_ql", [128, 8 * 384])
    w_ukv_d = din("w_ukv", [128, 2 * 1024])
    w_uq_d = din("w_uq", [128, 3 * 768])
    w_uqP_d = din("w_uqP", [128, 3 * 768])
    w_g_d = din("w_g", [16, 128, 8 * 128])
    w_bm_d = din("w_bm", [8, 128, 4 * 128])
    w_bd_d = din("w_bd", [8, 128, 4 * 128])
    w_o_d = din("w_o", [8, 128, 8 * 128])
    w_fg_d = din("w_fg", [22, 128, 8 * 128])
    w_fu_d = din("w_fu", [22, 128, 8 * 128])
    w_fd_d = din("w_fd", [8, 128, 22 * 128])
    y_d = nc.dram_tensor("y", [To, D], F32, kind="ExternalOutput").ap()

    es = contextlib.ExitStack()
    with es:
        def sem(name):
            return es.enter_context(nc.semaphore(name))

        PE = _E("pe", nc.tensor, sem("s_pe"), 1)
        ACT = _E("act", nc.scalar, sem("s_act"), 1)
        DVE = _E("dve", nc.vector, sem("s_dve"), 1)
        POOL = _E("pool", nc.gpsimd, sem("s_pool"), 1)
        SP = _E("sp", nc.sync, sem("s_sp"), 1)
        NDQ = 40
        DQ = [_E("dq%d" % i, None, sem("s_dq%d" % i), 16) for i in range(NDQ)]
        dq_ctr = [0]

        def _deps(E, reads, writes, pe_acc=False):
            deps = {}
            def add(t):
                if t is None:
                    return
                e2, v = t
                if deps.get(e2.name, (None, 0))[1] < v:
                    deps[e2.name] = (e2, v)
            for b in reads:
                add(b.w)
            for b in writes:
                add(b.w)
                for t in b.r.values():
                    add(t)
            for name, (e2, v) in deps.items():
                if e2 is E and E is PE:
                    continue
                if E.waited.get(name, 0) < v:
                    E.eng.wait_ge(e2.sem, v)
                    E.waited[name] = v

        def _mark(tk, reads, writes):
            for b in reads:
                e2, v = tk
                if b.r.get(e2.name, (None, 0))[1] < v:
                    b.r[e2.name] = tk
            for b in writes:
                b.w = tk
                b.r = {}

        def op(E, fn, reads=(), writes=()):
            _deps(E, reads, writes)
            ins = fn()
            ins.then_inc(E.sem, 1)
            E.n += 1
            _mark((E, E.n), reads, writes)

        def dma(fn, reads=(), writes=()):
            _deps(SP, reads, writes)
            q = DQ[dq_ctr[0] % NDQ]
            dq_ctr[0] += 1
            if q.n > 0 and SP.waited.get(q.name, 0) < q.n:
                nc.sync.wait_ge(q.sem, q.n)
                SP.waited[q.name] = q.n
            ins = fn()
            ins.then_inc(q.sem, 16)
            q.n += 16
            _mark((q, q.n), reads, writes)
            return (q, q.n)

        def barrier():
            srcs = [PE, ACT, DVE, POOL] + [q for q in DQ if q.n > 0]
            for E in (PE, ACT, DVE, POOL, SP):
                for e2 in srcs:
                    if e2.n > 0 and E.waited.get(e2.name, 0) < e2.n:
                        E.eng.wait_ge(e2.sem, e2.n)
                        E.waited[e2.name] = e2.n

        class T:
            def __init__(self, name, shape, dt, psum=False):
                uid[0] += 1
                name = "t%d_%s" % (uid[0], name)
                if psum:
                    self.t = es_cur[0].enter_context(nc.psum_tensor(name, list(shape), dt))
                else:
                    self.t = es_cur[0].enter_context(nc.sbuf_tensor(name, list(shape), dt))
                self.b = _B()

        es_cur = [es]
        uid = [0]

        cst = T("cst", [128, NCST], F32)
        cc = T("cc", [128, 4], F32)
        ident_f = T("ident_f", [128, 128], F32)
        ident_b = T("ident_b", [128, 128], BF16)
        ones_f = T("ones_f", [128, 128], F32)
        ones_b = T("ones_b", [128, 128], BF16)
        maskT_b = T("maskT_b", [128, 128], BF16)
        ipad_b = T("ipad_b", [128, 32], BF16)
        lam = T("lam", [128, 4], F32)
        ctmp = T("ctmp", [128, 128], F32)
        junk = es.enter_context(nc.sbuf_tensor("t_junk", [128, 1024], BF16))
        wst = [T("wst%d" % i, [128, WSTAGE], F32) for i in range(2)]
        wst_ctr = [0]

        dma(lambda: nc.sync.dma_start(out=cst.t[:], in_=cst_d[:]), writes=[cst.b])
        dma(lambda: nc.sync.dma_start(out=ident_f.t[:], in_=ident_d[:]), writes=[ident_f.b])
        op(DVE, lambda: nc.vector.memset(cc.t[:, 0:1], EPS), writes=[cc.b])
        op(DVE, lambda: nc.vector.memset(cc.t[:, 1:2], math.pi / 2.0), writes=[cc.b])
        op(DVE, lambda: nc.vector.memset(ones_f.t[:], 1.0), writes=[ones_f.b])
        op(DVE, lambda: nc.vector.memset(ones_b.t[:], 1.0), writes=[ones_b.b])
        op(DVE, lambda: nc.vector.tensor_copy(out=ident_b.t[:], in_=ident_f.t[:]), reads=[ident_f.b], writes=[ident_b.b])
        dma(lambda: nc.sync.dma_start(out=ctmp.t[:], in_=maskT_d[:]), writes=[ctmp.b])
        op(DVE, lambda: nc.vector.tensor_copy(out=maskT_b.t[:], in_=ctmp.t[:]), reads=[ctmp.b], writes=[maskT_b.b])
        dma(lambda: nc.sync.dma_start(out=ctmp.t[:, 0:32], in_=ipad_d[:]), writes=[ctmp.b])
        op(DVE, lambda: nc.vector.tensor_copy(out=ipad_b.t[:], in_=ctmp.t[:, 0:32]), reads=[ctmp.b], writes=[ipad_b.b])
        for i, (a, b_) in enumerate(((C_LQ1, C_LK1), (C_LQ2, C_LK2))):
            op(DVE, lambda a=a, b_=b_: nc.vector.tensor_tensor(out=ctmp.t[:, 0:64], in0=cst.t[:, a:a + 64], in1=cst.t[:, b_:b_ + 64], op=ALU.mult),
               reads=[cst.b], writes=[ctmp.b])
            op(DVE, lambda i=i: nc.vector.reduce_sum(out=lam.t[:, i:i + 1], in_=ctmp.t[:, 0:64], axis=mybir.AxisListType.X),
               reads=[ctmp.b], writes=[lam.b])
        op(ACT, lambda: nc.scalar.activation(out=lam.t[:, 0:2], in_=lam.t[:, 0:2], func=AF.Exp), reads=[lam.b], writes=[lam.b])
        op(DVE, lambda: nc.vector.tensor_tensor(out=lam.t[:, 2:3], in0=lam.t[:, 1:2], in1=lam.t[:, 0:1], op=ALU.subtract), reads=[lam.b], writes=[lam.b])
        op(DVE, lambda: nc.vector.tensor_scalar(out=lam.t[:, 2:3], in0=lam.t[:, 2:3], scalar1=1.0, scalar2=-0.2, op0=ALU.mult, op1=ALU.add), reads=[lam.b], writes=[lam.b])
        op(DVE, lambda: nc.vector.tensor_scalar(out=lam.t[:, 3:4], in0=cst.t[:, CG_SUB:CG_SUB + 1], scalar1=0.8, scalar2=0.0, op0=ALU.mult, op1=ALU.add), reads=[cst.b, lam.b], writes=[lam.b])

        cast_mode = ["pool"]
        cast_ctr = [0]

        def load_w(dst, dst_ap, src_ap, n):
            off = 0
            while off < n:
                m = min(WSTAGE, n - off)
                st = wst[wst_ctr[0] % len(wst)]
                wst_ctr[0] += 1
                dma(lambda st=st, off=off, m=m: nc.sync.dma_start(out=st.t[:, 0:m], in_=src_ap[:, off:off + m]), writes=[st.b])
                if cast_mode[0] == "pool":
                    op(POOL, lambda st=st, off=off, m=m: nc.gpsimd.tensor_copy(out=dst_ap[:, off:off + m], in_=st.t[:, 0:m]), reads=[st.b], writes=[dst.b])
                else:
                    k = cast_ctr[0] % 2
                    cast_ctr[0] += 1
                    if k == 0:
                        op(ACT, lambda st=st, off=off, m=m: nc.scalar.copy(out=dst_ap[:, off:off + m], in_=st.t[:, 0:m]), reads=[st.b], writes=[dst.b])
                    else:
                        op(DVE, lambda st=st, off=off, m=m: nc.vector.tensor_copy(out=dst_ap[:, off:off + m], in_=st.t[:, 0:m]), reads=[st.b], writes=[dst.b])
                off += m

        o_mla = T("o_mla", [128, 4, To], BF16)
        o_diff = T("o_diff", [128, 4, To], BF16)
        Osb = T("Osb", [128, 512], F32)
        sumsb = T("sumsb", [1, 512], F32)
        o1n = T("o1n", [128, 512], F32)
        dtmp = T("dtmp", [128, 512], F32)
        dtmp2 = T("dtmp2", [128, 512], F32)
        Pb = [T("Pb%d" % i, [128, 512], BF16) for i in range(4)]
        accP = T("accP", [128, 512], F32)
        Ctab = T("Ctab", [128, 512], F32)
        Stab = T("Stab", [128, 512], F32)
        rt1 = dtmp
        rt2 = dtmp2
        recb = dtmp2
        tile_ctr = [0]

        class XPc:
            pass
        XP = XPc()

        def alloc_xp(tables_only=False, nx=3):
            XP.nx = nx
            if not tables_only:
                XP.xin = [T("xin%d" % i, [128, D], F32) for i in range(nx)]
                XP.xs_bf = [T("xs_bf%d" % i, [128, D], BF16) for i in range(2)]
                XP.xnT = [T("xnT%d" % i, [128, 8, 512], BF16) for i in range(2)]
                XP.ssq = [T("ssq%d" % i, [128, 2], F32) for i in range(nx)]
            XP.posi = T("posi", [128, 512], I32)
            XP.tb_a = T("tb_a", [128, 512], F32)
            XP.tb_t = T("tb_t", [128, 512], F32)
            XP.tb_k = T("tb_k", [128, 512], I32)
            XP.tb_kf = T("tb_kf", [128, 512], F32)
            XP.tb_kf2 = T("tb_kf2", [128, 512], F32)

        def make_tables(pos_src, col0, c_if, c_sg, p0, p1, Cout, Sout, ccol0=0):
            posi, tb_a, tb_t, tb_k, tb_kf = XP.posi, XP.tb_a, XP.tb_t, XP.tb_k, XP.tb_kf
            posf = tb_t
            tb_r = tb_kf
            dma(lambda: nc.sync.dma_start(out=posi.t[p0:p1, :], in_=pos_src[p0:p1, col0:col0 + 512]), writes=[posi.b])
            op(DVE, lambda: nc.vector.tensor_copy(out=posf.t[p0:p1, :], in_=posi.t[p0:p1, :]), reads=[posi.b], writes=[posf.b])
            op(DVE, lambda: nc.vector.tensor_scalar_mul(tb_a.t[p0:p1, :], posf.t[p0:p1, :], cst.t[p0:p1, c_if:c_if + 1]),
               reads=[posf.b, cst.b], writes=[tb_a.b])
            for which in (0, 1):
                tb_kf = XP.tb_kf if which == 0 else XP.tb_kf2
                tb_r = tb_kf
                sh = 0.0 if which == 0 else 0.25
                op(DVE, lambda sh=sh: nc.vector.tensor_scalar(out=tb_t.t[p0:p1, :], in0=tb_a.t[p0:p1, :], scalar1=INV2PI, scalar2=sh, op0=ALU.mult, op1=ALU.add),
                   reads=[tb_a.b], writes=[tb_t.b])
                op(DVE, lambda: nc.vector.tensor_copy(out=tb_k.t[p0:p1, :], in_=tb_t.t[p0:p1, :]), reads=[tb_t.b], writes=[tb_k.b])
                op(DVE, lambda: nc.vector.tensor_copy(out=tb_kf.t[p0:p1, :], in_=tb_k.t[p0:p1, :]), reads=[tb_k.b], writes=[tb_kf.b])
                op(DVE, lambda: nc.vector.scalar_tensor_tensor(out=tb_t.t[p0:p1, :], in0=tb_kf.t[p0:p1, :], scalar=-C1, in1=tb_a.t[p0:p1, :], op0=ALU.mult, op1=ALU.add),
                   reads=[tb_kf.b, tb_a.b], writes=[tb_t.b])
                op(DVE, lambda: nc.vector.scalar_tensor_tensor(out=tb_r.t[p0:p1, :], in0=tb_kf.t[p0:p1, :], scalar=-C2, in1=tb_t.t[p0:p1, :], op0=ALU.mult, op1=ALU.add),
                   reads=[tb_kf.b, tb_t.b], writes=[tb_r.b])
                lo = -PI_LO if which == 0 else -PI_LO - math.pi / 2.0
                hi = PI_LO if which == 0 else PI_LO - math.pi / 2.0
                op(DVE, lambda lo=lo, hi=hi: nc.vector.tensor_scalar(out=tb_r.t[p0:p1, :], in0=tb_r.t[p0:p1, :], scalar1=hi, scalar2=lo, op0=ALU.min, op1=ALU.max),
                   reads=[tb_r.b], writes=[tb_r.b])
                if which == 0:
                    op(ACT, lambda: nc.scalar.activation(out=Sout.t[p0:p1, ccol0:ccol0 + 512], in_=tb_r.t[p0:p1, :], func=AF.Sin, scale=cst.t[p0:p1, c_sg:c_sg + 1]),
                       reads=[tb_r.b, cst.b], writes=[Sout.b])
                else:
                    op(ACT, lambda: nc.scalar.activation(out=Cout.t[p0:p1, ccol0:ccol0 + 512], in_=tb_r.t[p0:p1, :], func=AF.Sin, bias=cc.t[p0:p1, 1:2]),
                       reads=[tb_r.b, cc.b], writes=[Cout.b])

        def xpass_group(xsrc, g, pt, gi, per_tile=None):
            xo = XP.xnT[gi % 2]
            xin, ssq, xs_bf = XP.xin, XP.ssq, XP.xs_bf
            for j in range(4):
                k = tile_ctr[0]
                tile_ctr[0] += 1
                xi = xin[k % XP.nx]
                sq = ssq[k % XP.nx]
                xs = xs_bf[k % 2]
                r0 = g * 512 + j * 128
                dma(lambda xi=xi, r0=r0: nc.sync.dma_start(out=xi.t[:], in_=xsrc[r0:r0 + 128, :]), writes=[xi.b])
                op(ACT, lambda xi=xi, sq=sq: nc.scalar.activation(out=junk[:, :], in_=xi.t[:], func=AF.Square, accum_out=sq.t[:, 0:1]),
                   reads=[xi.b], writes=[sq.b])
                op(ACT, lambda sq=sq: nc.scalar.activation(out=sq.t[:, 1:2], in_=sq.t[:, 0:1], func=AF.Sqrt, scale=1.0 / D, bias=cc.t[:, 0:1]),
                   reads=[sq.b, cc.b], writes=[sq.b])
                op(DVE, lambda sq=sq: nc.vector.reciprocal(out=sq.t[:, 1:2], in_=sq.t[:, 1:2]), reads=[sq.b], writes=[sq.b])
                op(DVE, lambda xi=xi, sq=sq, xs=xs: nc.vector.tensor_scalar_mul(xs.t[:], xi.t[:], sq.t[:, 1:2]),
                   reads=[xi.b, sq.b], writes=[xs.b])
                for c in range(8):
                    p = pt[c // 4]
                    op(PE, lambda c=c, j=j, p=p, xs=xs: nc.tensor.transpose(out=p.t[:, c % 4, j * 128:(j + 1) * 128], in_=xs.t[:, c * 128:(c + 1) * 128], identity=ident_b.t[:]),
                       reads=[xs.b, ident_b.b], writes=[p.b])
                if per_tile is not None:
                    per_tile(j, xi)
            for c in range(8):
                p = pt[c // 4]
                if c % 2 == 0:
                    op(ACT, lambda c=c, p=p: nc.scalar.mul(out=xo.t[:, c, :], in_=p.t[:, c % 4, :], mul=cst.t[:, CG_MIX + c:CG_MIX + c + 1]),
                       reads=[p.b, cst.b], writes=[xo.b])
                else:
                    op(DVE, lambda c=c, p=p: nc.vector.tensor_scalar_mul(xo.t[:, c, :], p.t[:, c % 4, :], cst.t[:, CG_MIX + c:CG_MIX + c + 1]),
                       reads=[p.b, cst.b], writes=[xo.b])
            return xo

        gi_glob = [0]

        def pipelined(xsrc, n, pt, tables_fn, stage2_fn):
            xs = [None] * n
            xs[0] = xpass_group(xsrc, 0, pt, gi_glob[0])
            gi_glob[0] += 1
            for g in range(n):
                if g + 1 < n:
                    xs[g + 1] = xpass_group(xsrc, g + 1, pt, gi_glob[0])
                    gi_glob[0] += 1
                if tables_fn is not None:
                    tables_fn(g)
                stage2_fn(g, xs[g])

        def proj_fm(ps, xo, w, wcols, col0, m, nk=8):
            for c in range(nk):
                op(PE, lambda c=c: nc.tensor.matmul(ps.t[0:m, :], lhsT=w.t[:, c, col0:col0 + m], rhs=xo.t[:, c, :], start=(c == 0), stop=(c == nk - 1)),
                   reads=[w.b, xo.b], writes=[ps.b])

        def rope_combine(psA, psB, Ct, St, p0, p1, out_t, out_ap, tcol0=0, out_aps=None):
            op(DVE, lambda: nc.vector.tensor_tensor(out=rt1.t[p0:p1, :], in0=psA.t[p0:p1, :], in1=Ct.t[p0:p1, tcol0:tcol0 + 512], op=ALU.mult),
               reads=[psA.b, Ct.b], writes=[rt1.b])
            op(DVE, lambda: nc.vector.tensor_tensor(out=rt2.t[p0:p1, :], in0=psB.t[p0:p1, :], in1=St.t[p0:p1, tcol0:tcol0 + 512], op=ALU.mult),
               reads=[psB.b, St.b], writes=[rt2.b])
            if out_aps is None:
                out_aps = [(p0, p1, out_ap)]
            for (a0, a1, oap) in out_aps:
                op(POOL, lambda a0=a0, a1=a1, oap=oap: nc.gpsimd.tensor_tensor(out=oap, in0=rt1.t[a0:a1, :], in1=rt2.t[a0:a1, :], op=ALU.add),
                   reads=[rt1.b, rt2.b], writes=[out_t.b])

        def attention(KT, kt_ap, QT, q_ap, Vt, v_ap, dkrows, dvp, sumsep, s, ps_S, ps_O, ps_sum, scale, mb=0, ps_bc=None, mpad=None):
            units = []
            for kt in range(16 * s + 16):
                t = kt - 16 * s
                units.append((kt, 32 * t if t > 0 else 0, t >= 0))
            n = len(units)

            def QK(u):
                kt, c0, diag = units[u]
                S_ = ps_S[u % len(ps_S)]
                op(PE, lambda: nc.tensor.matmul(S_.t[:, c0:512], lhsT=kt_ap(kt), rhs=q_ap(c0), start=True, stop=(not diag)),
                   reads=[KT.b, QT.b], writes=[S_.b])
                if diag:
                    op(PE, lambda: nc.tensor.matmul(S_.t[:, c0:c0 + 32], lhsT=maskT_b.t[mb:mb + dkrows, :], rhs=ipad_b.t[mb:mb + dkrows, :], start=False, stop=True),
                       reads=[maskT_b.b, ipad_b.b], writes=[S_.b])

            def EXP(u):
                kt, c0, diag = units[u]
                S_ = ps_S[u % len(ps_S)]
                P_ = Pb[u % 4]
                op(ACT, lambda: nc.scalar.activation(out=P_.t[:, c0:512], in_=S_.t[:, c0:512], func=AF.Exp, scale=scale),
                   reads=[S_.b], writes=[P_.b])

            def PV(u):
                kt, c0, diag = units[u]
                P_ = Pb[u % 4]
                mo = mpad if mpad is not None else dvp
                op(PE, lambda: nc.tensor.matmul(ps_O.t[0:mo, c0:512], lhsT=v_ap(kt), rhs=P_.t[:, c0:512], start=(u == 0), stop=(u == n - 1)),
                   reads=[Vt.b, P_.b], writes=[ps_O.b])
                if sumsep:
                    if u % 2 == 1:
                        op(DVE, lambda: nc.vector.tensor_tensor(out=accP.t[:, c0:512], in0=accP.t[:, c0:512], in1=P_.t[:, c0:512], op=ALU.add),
                           reads=[P_.b, accP.b], writes=[accP.b])
                    else:
                        op(PE, lambda: nc.tensor.matmul(ps_sum.t[:, c0:512], lhsT=ones_b.t[:, :], rhs=P_.t[:, c0:512], start=(u == 0), stop=False),
                           reads=[ones_b.b, P_.b], writes=[ps_sum.b])

            if sumsep:
                op(POOL, lambda: nc.gpsimd.memset(accP.t[:, :], 0.0), writes=[accP.b])
            LA = len(ps_S) - 1
            for u in range(min(LA, n)):
                QK(u)
            EXP(0)
            for u in range(n):
                if u + LA < n:
                    QK(u + LA)
                if u + 1 < n:
                    EXP(u + 1)
                PV(u)
            nd = dvp if sumsep else dvp - 1
            op(DVE, lambda: nc.vector.tensor_copy(out=Osb.t[0:nd, :], in_=ps_O.t[0:nd, :]), reads=[ps_O.b], writes=[Osb.b])
            bc = ps_bc if ps_bc is not None else ps_S[0]
            if sumsep:
                op(PE, lambda: nc.tensor.matmul(ps_sum.t[:, :], lhsT=ones_f.t[:, :], rhs=accP.t[:, :], start=False, stop=True),
                   reads=[ones_f.b, accP.b], writes=[ps_sum.b])
                op(DVE, lambda: nc.vector.reciprocal(out=recb.t[:, :], in_=ps_sum.t[:, :]), reads=[ps_sum.b], writes=[recb.b])
                return recb, nd
            op(DVE, lambda: nc.vector.tensor_copy(out=sumsb.t[0:1, :], in_=ps_O.t[nd:nd + 1, :]), reads=[ps_O.b], writes=[sumsb.b])
            op(DVE, lambda: nc.vector.reciprocal(out=sumsb.t[0:1, :], in_=sumsb.t[0:1, :]), reads=[sumsb.b], writes=[sumsb.b])
            op(PE, lambda: nc.tensor.matmul(bc.t[0:nd, :], lhsT=ones_f.t[0:1, 0:nd], rhs=sumsb.t[0:1, :], start=True, stop=True),
               reads=[ones_f.b, sumsb.b], writes=[bc.b])
            return bc, nd

        for hp in range(2):
            ph = contextlib.ExitStack()
            barrier()
            es_cur[0] = ph
            with ph:
                KTd = T("KTd", [128, 2, S], BF16)
                Vd = T("Vd", [128, NT, 256], BF16)
                QTd = T("QTd", [128, 2, 2, To], BF16)
                wA = T("wA", [128, 8, 256], BF16)
                wB = T("wB", [128, 8, 256], BF16)
                wV = T("wV", [128, 8, 256], BF16)
                px = contextlib.ExitStack()
                es_cur[0] = px
                pt = [T("pt%d" % i, [128, 4, 512], BF16, psum=True) for i in range(2)]
                pj = [T("pj%d" % i, [128, 512], F32, psum=True) for i in range(4)]
                alloc_xp(nx=4)
                load_w(wA, wA.t[:].rearrange("p c n -> p (c n)"), w_dq_d[hp], 8 * 256)
                load_w(wB, wB.t[:].rearrange("p c n -> p (c n)"), w_dqP_d[hp], 8 * 256)
                op(POOL, lambda: nc.gpsimd.memset(QTd.t[:].rearrange("p a b n -> p (a b n)"), 0.0), writes=[QTd.b])

                def q_stage2(g, xo):
                    for hh in range(2):
                        proj_fm(pj[0], xo, wA, 256, hh * 128, 128)
                        proj_fm(pj[1], xo, wB, 256, hh * 128, 128)
                        rope_combine(pj[0], pj[1], Ctab, Stab, 0, 128, QTd, None,
                                     out_aps=[(0, 64, QTd.t[0:64, hh, 0, g * 512:(g + 1) * 512]), (64, 128, QTd.t[64:128, hh, 1, g * 512:(g + 1) * 512])])
                pipelined(x_own, Go, pt, lambda g: make_tables(posb_own, g * 512, C_IFD, C_SGD, 0, 128, Ctab, Stab), q_stage2)
                load_w(wA, wA.t[:].rearrange("p c n -> p (c n)"), w_dk_d[hp], 8 * 256)
                load_w(wB, wB.t[:].rearrange("p c n -> p (c n)"), w_dkP_d[hp], 8 * 256)
                load_w(wV, wV.t[:].rearrange("p c n -> p (c n)"), w_dv_d[hp], 8 * 256)

                def k_stage2(g, xo):
                    for hh in range(2):
                        proj_fm(pj[0], xo, wA, 256, hh * 128, 128)
                        proj_fm(pj[1], xo, wB, 256, hh * 128, 128)
                        rope_combine(pj[0], pj[1], Ctab, Stab, 0, 128, KTd, KTd.t[:, hh, g * 512:(g + 1) * 512])
                    for j in range(4):
                        pv = pj[2 + (j % 2)]
                        for c in range(8):
                            op(PE, lambda c=c, j=j, pv=pv: nc.tensor.matmul(pv.t[:, 0:256], lhsT=xo.t[:, c, j * 128:(j + 1) * 128], rhs=wV.t[:, c, :], start=(c == 0), stop=(c == 7)),
                               reads=[xo.b, wV.b], writes=[pv.b])
                        op(ACT, lambda j=j, pv=pv, g=g: nc.scalar.copy(out=Vd.t[:, g * 4 + j, :], in_=pv.t[:, 0:256]), reads=[pv.b], writes=[Vd.b])
                pipelined(x_full, G, pt, lambda g: make_tables(posb_full, g * 512, C_IFD, C_SGD, 0, 128, Ctab, Stab), k_stage2)
                barrier()
                px.close()
                py = contextlib.ExitStack()
                es_cur[0] = py
                ps_S = [T("dS%d" % i, [128, 512], F32, psum=True) for i in range(3)]
                ps_Od = [T("dO%d" % i, [128, 512], F32, psum=True) for i in range(2)]
                ps_sumd = T("dsum", [128, 512], F32, psum=True)
                psL = T("dL", [128, 512], F32, psum=True)
                dseq = [0]
                for hh in range(2):
                    h = hp * 2 + hh
                    for s in range(Go):
                        for comp in range(2):
                            b0 = comp * 64
                            ps_O_ = ps_Od[dseq[0] % 2]
                            dseq[0] += 1
                            bc, nd = attention(
                                KTd, lambda kt: KTd.t[:, hh, kt * 128:(kt + 1) * 128],
                                QTd, lambda c0: QTd.t[:, hh, comp, s * 512 + c0:(s + 1) * 512],
                                Vd, lambda kt: Vd.t[:, kt, hh * 128:(hh + 1) * 128],
                                128, 128, True, s, ps_S, ps_O_, ps_sumd, 0.125, mb=0)
                            if comp == 0:
                                op(DVE, lambda: nc.vector.tensor_tensor(out=o1n.t[:], in0=Osb.t[:], in1=bc.t[:], op=ALU.mult),
                                   reads=[Osb.b, bc.b], writes=[o1n.b])
                            else:
                                op(DVE, lambda: nc.vector.tensor_tensor(out=dtmp.t[:], in0=Osb.t[:], in1=bc.t[:], op=ALU.mult),
                                   reads=[Osb.b, bc.b], writes=[dtmp.b])
                                op(DVE, lambda: nc.vector.scalar_tensor_tensor(out=dtmp.t[:], in0=dtmp.t[:], scalar=lam.t[:, 2:3], in1=o1n.t[:], op0=ALU.mult, op1=ALU.add),
                                   reads=[dtmp.b, lam.b, o1n.b], writes=[dtmp.b])
                                op(DVE, lambda: nc.vector.tensor_tensor(out=dtmp2.t[:], in0=dtmp.t[:], in1=dtmp.t[:], op=ALU.mult),
                                   reads=[dtmp.b], writes=[dtmp2.b])
                                op(PE, lambda: nc.tensor.matmul(psL.t[:], lhsT=ones_f.t[:], rhs=dtmp2.t[:], start=True, stop=True),
                                   reads=[ones_f.b, dtmp2.b], writes=[psL.b])
                                op(ACT, lambda: nc.scalar.activation(out=dtmp2.t[:], in_=psL.t[:], func=AF.Sqrt, scale=1.0 / 128, bias=cc.t[:, 0:1]),
                                   reads=[psL.b, cc.b], writes=[dtmp2.b])
                                op(DVE, lambda: nc.vector.reciprocal(out=dtmp2.t[:], in_=dtmp2.t[:]), reads=[dtmp2.b], writes=[dtmp2.b])
                                op(DVE, lambda: nc.vector.scalar_tensor_tensor(out=o_diff.t[:, h, s * 512:(s + 1) * 512], in0=dtmp.t[:], scalar=lam.t[:, 3:4], in1=dtmp2.t[:], op0=ALU.mult, op1=ALU.mult),
                                   reads=[dtmp.b, lam.b, dtmp2.b], writes=[o_diff.b])
                barrier()
                py.close()
                es_cur[0] = ph
            barrier()
        es_cur[0] = es

        ph = contextlib.ExitStack()
        es_cur[0] = ph
        with ph:
            kvnT = T("kvnT", [128, 2, S], BF16)
            KTm = [T("KTm%d" % i, [96, S], BF16) for i in range(2)]
            qnT = T("qnT", [128, 3, To], BF16)
            warena = T("warena", [128, 6656], BF16)

            class _V:
                def __init__(self, off, c, n):
                    self.t = warena.t[:, off:off + c * n].rearrange("p (c n) -> p c n", c=c)
                    self.flat = warena.t[:, off:off + c * n]
                    self.b = warena.b
            w_ql = _V(0, 8, 384)
            w_kvl = _V(0, 8, 256)
            w_kr = _V(2048, 8, 96)
            w_krP = _V(2048 + 768, 8, 96)
            w_ukv = _V(0, 2, 1024)
            w_uq = _V(2048, 3, 768)
            w_uqP = _V(2048 + 2304, 3, 768)

            pa = contextlib.ExitStack()
            es_cur[0] = pa
            with pa:
                alloc_xp()
                lat = T("lat", [128, 3, 512], F32)
                latsq = T("latsq", [128, 512], BF16)
                rsb = Osb
                pt = [T("ptm%d" % i, [128, 4, 512], BF16, psum=True) for i in range(2)]
                pj = [T("pjm%d" % i, [128, 512], F32, psum=True) for i in range(4)]

                def latent_norm(xo, w, nch, gcol, out_t, out_col0):
                    for ch in range(nch):
                        p = pj[ch % 2]
                        proj_fm(p, xo, w, nch * 128, ch * 128, 128)
                        op(ACT, lambda ch=ch, p=p: nc.scalar.copy(out=lat.t[:, ch, :], in_=p.t[:]), reads=[p.b], writes=[lat.b])
                        op(DVE, lambda ch=ch: nc.vector.tensor_tensor(out=latsq.t[:], in0=lat.t[:, ch, :], in1=lat.t[:, ch, :], op=ALU.mult),
                           reads=[lat.b], writes=[latsq.b])
                        op(PE, lambda ch=ch: nc.tensor.matmul(pj[2].t[:], lhsT=ones_b.t[:], rhs=latsq.t[:], start=(ch == 0), stop=(ch == nch - 1)),
                           reads=[ones_b.b, latsq.b], writes=[pj[2].b])
                    op(ACT, lambda: nc.scalar.activation(out=rsb.t[:], in_=pj[2].t[:], func=AF.Sqrt, scale=1.0 / (nch * 128), bias=cc.t[:, 0:1]),
                       reads=[pj[2].b, cc.b], writes=[rsb.b])
                    op(DVE, lambda: nc.vector.reciprocal(out=rsb.t[:], in_=rsb.t[:]), reads=[rsb.b], writes=[rsb.b])
                    for ch in range(nch):
                        op(DVE, lambda ch=ch: nc.vector.scalar_tensor_tensor(out=out_t.t[:, ch, out_col0:out_col0 + 512], in0=lat.t[:, ch, :], scalar=cst.t[:, gcol + ch:gcol + ch + 1], in1=rsb.t[:], op0=ALU.mult, op1=ALU.mult),
                           reads=[lat.b, cst.b, rsb.b], writes=[out_t.b])

                load_w(w_ql, w_ql.flat, w_ql_d, 8 * 384)
                pipelined(x_own, Go, pt, None, lambda g, xo: latent_norm(xo, w_ql, 3, CG_Q, qnT, g * 512))
                load_w(w_kvl, w_kvl.flat, w_kvl_d, 8 * 256)
                load_w(w_kr, w_kr.flat, w_kr_d, 8 * 96)
                load_w(w_krP, w_krP.flat, w_krP_d, 8 * 96)

                def mk_stage2(g, xo):
                    latent_norm(xo, w_kvl, 2, CG_KV, kvnT, g * 512)
                    proj_fm(pj[0], xo, w_kr, 96, 0, 96)
                    proj_fm(pj[1], xo, w_krP, 96, 0, 96)
                    rope_combine(pj[0], pj[1], Ctab, Stab, 64, 96, KTm[0], KTm[0].t[64:96, g * 512:(g + 1) * 512])
                    op(POOL, lambda g=g: nc.gpsimd.tensor_copy(out=KTm[1].t[64:96, g * 512:(g + 1) * 512], in_=KTm[0].t[64:96, g * 512:(g + 1) * 512]),
                       reads=[KTm[0].b], writes=[KTm[1].b])
                pipelined(x_full, G, pt, lambda g: make_tables(posb_full, g * 512, C_IFM, C_SGM, 64, 96, Ctab, Stab), mk_stage2)
            barrier()
            es_cur[0] = ph

            pb = contextlib.ExitStack()
            es_cur[0] = pb
            with pb:
                alloc_xp(tables_only=True)
                Vm = [T("Vm%d" % i, [128, NT * 65 + 64], BF16) for i in range(2)]
                QTm = [T("QTm%d" % i, [96, To], BF16) for i in range(2)]
                CqT = T("CqT", [96, To], F32)
                SqT = T("SqT", [96, To], F32)
                ps_S = [T("psS%d" % i, [128, 512], F32, psum=True) for i in range(3)]
                ps_Os = [T("psO%d" % i, [128, 512], F32, psum=True) for i in range(2)]
                ps_bc = T("psbc", [128, 512], F32, psum=True)
                pbk = [T("pbk%d" % i, [128, 512], F32, psum=True) for i in range(2)]
                seq_ctr = [0]
                load_w(w_ukv, w_ukv.flat, w_ukv_d, 2 * 1024)
                load_w(w_uq, w_uq.flat, w_uq_d, 3 * 768)
                load_w(w_uqP, w_uqP.flat, w_uqP_d, 3 * 768)
                for i in range(2):
                    op(DVE, lambda i=i: nc.vector.memset(Vm[i].t[:, :], 0.0), writes=[Vm[i].b])
                    op(DVE, lambda i=i: nc.vector.memset(Vm[i].t[:, 0:NT * 65].rearrange("p (a b) -> p a b", b=65)[:, :, 64:65], 1.0), writes=[Vm[i].b])
                for g in range(Go):
                    make_tables(posb_own, g * 512, C_IFM, C_SGM, 64, 96, CqT, SqT, ccol0=g * 512)
                bk_ctr = [0]

                def nextbank():
                    p = pbk[bk_ctr[0] % 2]
                    bk_ctr[0] += 1
                    return p

                def build_head(h):
                    KT_, V_, Q_ = KTm[h % 2], Vm[h % 2], QTm[h % 2]
                    for g in range(Go):
                        pA = nextbank()
                        pB = nextbank()
                        for ch in range(3):
                            op(PE, lambda ch=ch, g=g: nc.tensor.matmul(pA.t[0:96, :], lhsT=w_uq.t[:, ch, h * 96:(h + 1) * 96], rhs=qnT.t[:, ch, g * 512:(g + 1) * 512], start=(ch == 0), stop=(ch == 2)),
                               reads=[w_uq.b, qnT.b], writes=[pA.b])
                        for ch in range(3):
                            op(PE, lambda ch=ch, g=g: nc.tensor.matmul(pB.t[0:96, :], lhsT=w_uqP.t[:, ch, h * 96:(h + 1) * 96], rhs=qnT.t[:, ch, g * 512:(g + 1) * 512], start=(ch == 0), stop=(ch == 2)),
                               reads=[w_uqP.b, qnT.b], writes=[pB.b])
                        op(DVE, lambda g=g: nc.vector.tensor_copy(out=Q_.t[0:64, g * 512:(g + 1) * 512], in_=pA.t[0:64, :]), reads=[pA.b], writes=[Q_.b])
                        rope_combine(pA, pB, CqT, SqT, 64, 96, Q_, Q_.t[64:96, g * 512:(g + 1) * 512], tcol0=g * 512)
                    for g in range(G):
                        p = nextbank()
                        for ch in range(2):
                            op(PE, lambda ch=ch, g=g, p=p: nc.tensor.matmul(p.t[0:64, :], lhsT=w_ukv.t[:, ch, h * 128:h * 128 + 64], rhs=kvnT.t[:, ch, g * 512:(g + 1) * 512], start=(ch == 0), stop=(ch == 1)),
                               reads=[w_ukv.b, kvnT.b], writes=[p.b])
                        op(DVE, lambda g=g, p=p: nc.vector.tensor_copy(out=KT_.t[0:64, g * 512:(g + 1) * 512], in_=p.t[0:64, :]), reads=[p.b], writes=[KT_.b])
                    for t8 in range(NT // 8):
                        p = nextbank()
                        for tt in range(8):
                            tok = (t8 * 8 + tt) * 128
                            for ch in range(2):
                                op(PE, lambda ch=ch, tt=tt, tok=tok, p=p: nc.tensor.matmul(p.t[:, tt * 64:(tt + 1) * 64], lhsT=kvnT.t[:, ch, tok:tok + 128], rhs=w_ukv.t[:, ch, h * 128 + 64:h * 128 + 128], start=(ch == 0), stop=(ch == 1)),
                                   reads=[kvnT.b, w_ukv.b], writes=[p.b])
                        op(DVE, lambda t8=t8, p=p: nc.vector.tensor_copy(out=V_.t[:, t8 * 8 * 65:(t8 + 1) * 8 * 65].rearrange("p (a b) -> p a b", b=65)[:, :, 0:64], in_=p.t[:].rearrange("p (a b) -> p a b", a=8)), reads=[p.b], writes=[V_.b])

                build_head(0)
                for h in range(8):
                    if h + 1 < 8:
                        build_head(h + 1)
                    KT_, V_, Q_ = KTm[h % 2], Vm[h % 2], QTm[h % 2]
                    for s in range(Go):
                        ps_O = ps_Os[seq_ctr[0] % 2]
                        seq_ctr[0] += 1
                        bc, nd = attention(
                            KT_, lambda kt: KT_.t[0:96, kt * 128:(kt + 1) * 128],
                            Q_, lambda c0: Q_.t[0:96, s * 512 + c0:(s + 1) * 512],
                            V_, lambda kt: V_.t[:, kt * 65:kt * 65 + 128],
                            96, 65, False, s, ps_S, ps_O, None, 96.0 ** -0.5, ps_bc=ps_bc, mpad=128)
                        po = (h % 2) * 64
                        op(DVE, lambda: nc.vector.tensor_tensor(out=o_mla.t[po:po + 64, h // 2, s * 512:(s + 1) * 512], in0=Osb.t[0:64, :], in1=bc.t[0:64, :], op=ALU.mult),
                           reads=[Osb.b, bc.b], writes=[o_mla.b])
            barrier()
            es_cur[0] = ph
        barrier()
        es_cur[0] = es

        ph = contextlib.ExitStack()
        es_cur[0] = ph
        with ph:
            NSLOT = 5
            slots = [T("wsl%d" % i, [128, 22 * 128], BF16) for i in range(NSLOT)]
            sl_ctr = [0]
            xT = T("xT", [128, 8, 512], F32)
            sig = T("sig", [128, 2, 512], BF16)
            mt1 = dtmp
            mt2 = dtmp2
            merged = T("merged", [128, 8, 512], BF16)
            hn = T("hn", [128, 8, 512], BF16)
            hid = T("hid", [128, 22, 512], BF16)
            sqx = o1n
            rsx = T("rsx", [128, 512], F32)
            sg = Osb
            ysb = T("ysb", [128, D], F32)
            pt = [T("ptt%d" % i, [128, 4, 512], BF16, psum=True) for i in range(2)]
            pj = [T("pjt%d" % i, [128, 512], F32, psum=True) for i in range(4)]

            def wblock(src_ap, n):
                sl = slots[sl_ctr[0] % NSLOT]
                sl_ctr[0] += 1
                load_w(sl, sl.t[:, 0:n], src_ap, n)
                return sl

            def rms_bcast(src):
                for c in range(8):
                    op(DVE, lambda c=c: nc.vector.tensor_tensor(out=sqx.t[:], in0=src.t[:, c, :], in1=src.t[:, c, :], op=ALU.mult),
                       reads=[src.b], writes=[sqx.b])
                    op(PE, lambda c=c: nc.tensor.matmul(pj[3].t[:], lhsT=ones_f.t[:], rhs=sqx.t[:], start=(c == 0), stop=(c == 7)),
                       reads=[ones_f.b, sqx.b], writes=[pj[3].b])
                op(ACT, lambda: nc.scalar.activation(out=rsx.t[:], in_=pj[3].t[:], func=AF.Sqrt, scale=1.0 / D, bias=cc.t[:, 0:1]),
                   reads=[pj[3].b, cc.b], writes=[rsx.b])
                op(DVE, lambda: nc.vector.reciprocal(out=rsx.t[:], in_=rsx.t[:]), reads=[rsx.b], writes=[rsx.b])

            alloc_xp()
            cast_mode[0] = "actdve"
            wst.extend([T("wstx%d" % i, [128, WSTAGE], F32) for i in range(2)])
            jobs = []
            gi_box = [0]

            def add_group(g):
                tsl = slice(g * 512, (g + 1) * 512)
                st = {}

                def pre(_):
                    def xt_tile(j, xi):
                        for half in range(2):
                            p = pj[half]
                            for c4 in range(4):
                                c = half * 4 + c4
                                op(PE, lambda c=c, c4=c4, p=p: nc.tensor.transpose(out=p.t[:, c4 * 128:(c4 + 1) * 128], in_=xi.t[:, c * 128:(c + 1) * 128], identity=ident_f.t[:]),
                                   reads=[xi.b, ident_f.b], writes=[p.b])
                            op(DVE, lambda half=half, p=p: nc.vector.tensor_copy(out=xT.t[:, half * 4:(half + 1) * 4, j * 128:(j + 1) * 128], in_=p.t[:].rearrange("p (a b) -> p a b", a=4)),
                               reads=[p.b], writes=[xT.b])
                    st["xo"] = xpass_group(x_own, g, pt, gi_glob[0], per_tile=xt_tile)
                    gi_glob[0] += 1
                jobs.append((None, 0, pre))

                def gate(br, e):
                    def fn(wg):
                        xo = st["xo"]
                        p = pj[br]
                        for c in range(8):
                            op(PE, lambda c=c: nc.tensor.matmul(p.t[:], lhsT=wg.t[:, c * 128:(c + 1) * 128], rhs=xo.t[:, c, :], start=(c == 0), stop=(c == 7)),
                               reads=[wg.b, xo.b], writes=[p.b])
                        bcol = CB_GATE + br * 8 + e
                        op(ACT, lambda: nc.scalar.activation(out=sig.t[:, br, :], in_=p.t[:], func=AF.Sigmoid, bias=cst.t[:, bcol:bcol + 1]),
                           reads=[p.b, cst.b], writes=[sig.b])
                    return fn

                def bm(e):
                    def fn(wm):
                        for c in range(4):
                            op(PE, lambda c=c: nc.tensor.matmul(pj[2].t[:], lhsT=wm.t[:, c * 128:(c + 1) * 128], rhs=o_mla.t[:, c, tsl], start=(c == 0), stop=(c == 3)),
                               reads=[wm.b, o_mla.b], writes=[pj[2].b])
                    return fn

                def bd(e):
                    def fn(wd):
                        for c in range(4):
                            op(PE, lambda c=c: nc.tensor.matmul(pj[3].t[:], lhsT=wd.t[:, c * 128:(c + 1) * 128], rhs=o_diff.t[:, c, tsl], start=(c == 0), stop=(c == 3)),
                               reads=[wd.b, o_diff.b], writes=[pj[3].b])
                        op(DVE, lambda: nc.vector.tensor_tensor(out=mt1.t[:], in0=pj[2].t[:], in1=sig.t[:, 0, :], op=ALU.mult), reads=[pj[2].b, sig.b], writes=[mt1.b])
                        op(DVE, lambda: nc.vector.tensor_tensor(out=mt2.t[:], in0=pj[3].t[:], in1=sig.t[:, 1, :], op=ALU.mult), reads=[pj[3].b, sig.b], writes=[mt2.b])
                        op(POOL, lambda: nc.gpsimd.tensor_tensor(out=merged.t[:, e, :], in0=mt1.t[:], in1=mt2.t[:], op=ALU.add), reads=[mt1.b, mt2.b], writes=[merged.b])
                    return fn

                for e in range(8):
                    jobs.append((w_g_d[e], 8 * 128, gate(0, e)))
                    jobs.append((w_g_d[8 + e], 8 * 128, gate(1, e)))
                    jobs.append((w_bm_d[e], 4 * 128, bm(e)))
                    jobs.append((w_bd_d[e], 4 * 128, bd(e)))

                def outp(e):
                    def fn(wo):
                        p = pj[e % 2]
                        for c in range(8):
                            op(PE, lambda c=c: nc.tensor.matmul(p.t[:], lhsT=wo.t[:, c * 128:(c + 1) * 128], rhs=merged.t[:, c, :], start=(c == 0), stop=(c == 7)),
                               reads=[wo.b, merged.b], writes=[p.b])
                        op(DVE, lambda: nc.vector.tensor_tensor(out=xT.t[:, e, :], in0=xT.t[:, e, :], in1=p.t[:], op=ALU.add), reads=[xT.b, p.b], writes=[xT.b])
                    return fn
                for e in range(8):
                    jobs.append((w_o_d[e], 8 * 128, outp(e)))

                def ffn_norm(_):
                    rms_bcast(xT)
                    for c in range(8):
                        op(DVE, lambda c=c: nc.vector.scalar_tensor_tensor(out=hn.t[:, c, :], in0=xT.t[:, c, :], scalar=cst.t[:, CG_FFN + c:CG_FFN + c + 1], in1=rsx.t[:], op0=ALU.mult, op1=ALU.mult),
                           reads=[xT.b, cst.b, rsx.b], writes=[hn.b])
                jobs.append((None, 0, ffn_norm))

                def fgate(f):
                    def fn(w):
                        for c in range(8):
                            op(PE, lambda c=c: nc.tensor.matmul(pj[0].t[:], lhsT=w.t[:, c * 128:(c + 1) * 128], rhs=hn.t[:, c, :], start=(c == 0), stop=(c == 7)),
                               reads=[w.b, hn.b], writes=[pj[0].b])
                        op(ACT, lambda: nc.scalar.activation(out=sg.t[:], in_=pj[0].t[:], func=AF.Silu), reads=[pj[0].b], writes=[sg.b])
                    return fn

                def fup(f):
                    def fn(w):
                        for c in range(8):
                            op(PE, lambda c=c: nc.tensor.matmul(pj[1].t[:], lhsT=w.t[:, c * 128:(c + 1) * 128], rhs=hn.t[:, c, :], start=(c == 0), stop=(c == 7)),
                               reads=[w.b, hn.b], writes=[pj[1].b])
                        op(DVE, lambda: nc.vector.tensor_tensor(out=hid.t[:, f, :], in0=sg.t[:], in1=pj[1].t[:], op=ALU.mult), reads=[sg.b, pj[1].b], writes=[hid.b])
                    return fn
                for f in range(22):
                    jobs.append((w_fg_d[f], 8 * 128, fgate(f)))
                    jobs.append((w_fu_d[f], 8 * 128, fup(f)))

                def fdown(e):
                    def fn(w):
                        p = pj[2 + e % 2]
                        for f in range(22):
                            op(PE, lambda f=f: nc.tensor.matmul(p.t[:], lhsT=w.t[:, f * 128:(f + 1) * 128], rhs=hid.t[:, f, :], start=(f == 0), stop=(f == 21)),
                               reads=[w.b, hid.b], writes=[p.b])
                        op(DVE, lambda: nc.vector.tensor_tensor(out=xT.t[:, e, :], in0=xT.t[:, e, :], in1=p.t[:], op=ALU.add), reads=[xT.b, p.b], writes=[xT.b])
                    return fn
                for e in range(8):
                    jobs.append((w_fd_d[e], 22 * 128, fdown(e)))

                def post(_):
                    rms_bcast(xT)
                    for c in range(8):
                        op(DVE, lambda c=c: nc.vector.scalar_tensor_tensor(out=xT.t[:, c, :], in0=xT.t[:, c, :], scalar=cst.t[:, CG_FIN + c:CG_FIN + c + 1], in1=rsx.t[:], op0=ALU.mult, op1=ALU.mult),
                           reads=[xT.b, cst.b, rsx.b], writes=[xT.b])
                    for j in range(4):
                        for half in range(2):
                            p = pj[half]
                            for c4 in range(4):
                                c = half * 4 + c4
                                op(PE, lambda c=c, c4=c4, j=j, p=p: nc.tensor.transpose(out=p.t[:, c4 * 128:(c4 + 1) * 128], in_=xT.t[:, c, j * 128:(j + 1) * 128], identity=ident_f.t[:]),
                                   reads=[xT.b, ident_f.b], writes=[p.b])
                            op(ACT, lambda half=half, p=p: nc.scalar.copy(out=ysb.t[:, half * 512:(half + 1) * 512], in_=p.t[:]), reads=[p.b], writes=[ysb.b])
                        r0 = g * 512 + j * 128
                        dma(lambda r0=r0: nc.sync.dma_start(out=y_d[r0:r0 + 128, :], in_=ysb.t[:]), reads=[ysb.b])
                jobs.append((None, 0, post))

            for g in range(Go):
                add_group(g)
            PD = 3
            loaded = {}
            nj = len(jobs)

            def issue_load(i):
                src, n, fn = jobs[i]
                if src is not None:
                    loaded[i] = wblock(src, n)

            widx = [i for i in range(nj) if jobs[i][0] is not None]
            nxt = [0]

            def prefetch_upto(k):
                while nxt[0] < len(widx) and nxt[0] < k:
                    issue_load(widx[nxt[0]])
                    nxt[0] += 1
            wseen = 0
            for i in range(nj):
                src, n, fn = jobs[i]
                if src is not None:
                    prefetch_upto(wseen + 1 + PD)
                    wseen += 1
                    fn(loaded.pop(i))
                else:
                    fn(None)
            for q in DQ:
                if q.n > 0:
                    nc.sync.wait_ge(q.sem, q.n)
        barrier()
        es_cur[0] = es
    return nc


def _wl(W):
    Din, n = W.shape
    C = Din // 128
    return np.ascontiguousarray(W.reshape(C, 128, n).transpose(1, 0, 2).reshape(128, C * n)).astype(np.float32)


def _partner_cols(W, groups):
    P = np.zeros_like(W)
    for st, half in groups:
        P[:, st:st + half] = W[:, st + half:st + 2 * half]
        P[:, st + half:st + 2 * half] = W[:, st:st + half]
    return P


def _inv_freq(rot_dim):
    half = rot_dim // 2
    return np.exp((-math.log(THETA)) * np.arange(half, dtype=np.float32) * np.float32(2.0 / rot_dim)).astype(np.float32)


def prepare_weights(inp):
    w_in = np.asarray(inp["w_in"], np.float32)[0]
    d = {}
    dq = w_in[:, 672:1184]
    dk = w_in[:, 1184:1696]
    dv = w_in[:, 1696:2208]
    grp = [(i * 64, 8) for i in range(4)]
    d["w_dq"] = np.stack([_wl(dq[:, hp * 256:(hp + 1) * 256]) for hp in range(2)])
    d["w_dqP"] = np.stack([_wl(_partner_cols(dq[:, hp * 256:(hp + 1) * 256], grp)) for hp in range(2)])
    d["w_dk"] = np.stack([_wl(dk[:, hp * 256:(hp + 1) * 256]) for hp in range(2)])
    d["w_dkP"] = np.stack([_wl(_partner_cols(dk[:, hp * 256:(hp + 1) * 256], grp)) for hp in range(2)])
    d["w_dv"] = np.stack([_wl(dv[:, hp * 256:(hp + 1) * 256]) for hp in range(2)])
    d["w_kvl"] = _wl(w_in[:, 384:640])
    kr = np.zeros((D, 96), np.float32)
    kr[:, 64:96] = w_in[:, 640:672]
    d["w_kr"] = _wl(kr)
    d["w_krP"] = _wl(_partner_cols(kr, [(64, 16)]))
    d["w_ql"] = _wl(w_in[:, 0:384])
    d["w_ukv"] = _wl(np.asarray(inp["mla_w_ukv"], np.float32)[0])
    uq = np.asarray(inp["mla_w_uq"], np.float32)[0]
    d["w_uq"] = _wl(uq)
    d["w_uqP"] = _wl(_partner_cols(uq, [(h * 96 + 64, 16) for h in range(8)]))
    wg = w_in[:, 2208:4256]
    d["w_g"] = np.stack([_wl(wg[:, e * 128:(e + 1) * 128]) for e in range(16)])
    bm = np.asarray(inp["w_branch_mla"], np.float32)[0]
    bd = np.asarray(inp["w_branch_diff"], np.float32)[0]
    wo = np.asarray(inp["w_out"], np.float32)[0]
    d["w_bm"] = np.stack([_wl(bm[:, e * 128:(e + 1) * 128]) for e in range(8)])
    d["w_bd"] = np.stack([_wl(bd[:, e * 128:(e + 1) * 128]) for e in range(8)])
    d["w_o"] = np.stack([_wl(wo[:, e * 128:(e + 1) * 128]) for e in range(8)])
    fg = np.asarray(inp["w_ffn_gate"], np.float32)[0]
    fu = np.asarray(inp["w_ffn_up"], np.float32)[0]
    fd = np.asarray(inp["w_ffn_down"], np.float32)[0]
    d["w_fg"] = np.stack([_wl(fg[:, f * 128:(f + 1) * 128]) for f in range(22)])
    d["w_fu"] = np.stack([_wl(fu[:, f * 128:(f + 1) * 128]) for f in range(22)])
    d["w_fd"] = np.stack([_wl(fd[:, e * 128:(e + 1) * 128]) for e in range(8)])
    cst = np.zeros((128, NCST), np.float32)

    def pc(v, C):
        return np.asarray(v, np.float32).reshape(C, 128).T

    cst[:, CG_MIX:CG_MIX + 8] = pc(inp["norm_mix_g"][0], 8)
    cst[:, CB_GATE:CB_GATE + 16] = pc(inp["b_gate"][0], 16)
    cst[:, CG_Q:CG_Q + 3] = pc(inp["mla_q_norm_g"][0], 3)
    cst[:, CG_KV:CG_KV + 2] = pc(inp["mla_kv_norm_g"][0], 2)
    cst[:, CG_SUB:CG_SUB + 1] = pc(inp["diff_subln_g"][0], 1)
    cst[:, CG_FFN:CG_FFN + 8] = pc(inp["norm_ffn_g"][0], 8)
    cst[:, CG_FIN:CG_FIN + 8] = pc(inp["norm_final_g"], 8)
    fdiff = _inv_freq(16)
    fmla = _inv_freq(32)
    for p in range(128):
        j = p % 64
        if j < 8:
            cst[p, C_IFD], cst[p, C_SGD] = fdiff[j], -1.0
        elif j < 16:
            cst[p, C_IFD], cst[p, C_SGD] = fdiff[j - 8], 1.0
        if 64 <= p < 80:
            cst[p, C_IFM], cst[p, C_SGM] = fmla[p - 64], -1.0
        elif 80 <= p < 96:
            cst[p, C_IFM], cst[p, C_SGM] = fmla[p - 80], 1.0
    cst[:, C_LQ1:C_LQ1 + 64] = np.asarray(inp["diff_lambda_q1"], np.float32)[0][None, :]
    cst[:, C_LK1:C_LK1 + 64] = np.asarray(inp["diff_lambda_k1"], np.float32)[0][None, :]
    cst[:, C_LQ2:C_LQ2 + 64] = np.asarray(inp["diff_lambda_q2"], np.float32)[0][None, :]
    cst[:, C_LK2:C_LK2 + 64] = np.asarray(inp["diff_lambda_k2"], np.float32)[0][None, :]
    d["cst"] = cst
    d["ident"] = np.eye(128, dtype=np.float32)
    ip = np.zeros((128, 32), np.float32)
    ip[:32, :32] = np.eye(32, dtype=np.float32)
    ip[64:96, :32] = np.eye(32, dtype=np.float32)
    d["ipad"] = ip
    return d


def _mask_for(r):
    m = np.zeros((128, 128), np.float32)
    kk = np.arange(128)[None, :]
    ii = np.arange(32)[:, None]
    m[:32, :] = np.where(kk <= 4 * ii + r, 0.0, NEG)
    m[64:96, :] = m[:32, :]
    return m


_NC_CACHE = {}


def run(inputs, S):
    x = np.asarray(inputs["x"], np.float32)
    pos = np.asarray(inputs["positions"], np.int32)
    B = x.shape[0]
    wd = prepare_weights(inputs)
    if S not in _NC_CACHE:
        _NC_CACHE[S] = build(S)
    nc = _NC_CACHE[S]
    in_maps = []
    for c in range(8):
        b, r = c // 4, c % 4
        m = dict(wd)
        m["x_full"] = np.ascontiguousarray(x[b])
        m["x_own"] = np.ascontiguousarray(x[b, r::4])
        m["posb_full"] = np.ascontiguousarray(np.broadcast_to(pos[b][None, :], (128, S)))
        m["posb_own"] = np.ascontiguousarray(np.broadcast_to(pos[b, r::4][None, :], (128, S // 4)))
        m["maskT"] = _mask_for(r)
        in_maps.append(m)
    res = run_bass_kernel_spmd(nc, in_maps, core_ids=list(range(8)))
    out = np.zeros((B, S, D), np.float32)
    for c in range(8):
        b, r = c // 4, c % 4
        out[b, r::4] = res.results[c]["y"]
    return out


def kernel(**inputs):
    return run(inputs, 8192)
```

```python
import math
import contextlib
import numpy as np
import concourse.bass as bass
import concourse.mybir as mybir
from concourse.bass_utils import run_bass_kernel_spmd

F32 = mybir.dt.float32
BF16 = mybir.dt.bfloat16
I32 = mybir.dt.int32
AF = mybir.ActivationFunctionType
ALU = mybir.AluOpType

D = 1024
EPS = 1e-6
THETA = 500000.0
INV2PI = 1.0 / (2.0 * math.pi)
C1 = 6.28125
C2 = 2.0 * math.pi - 6.28125
NEG = -30000.0
PI_LO = 3.14159
CG_MIX, CB_GATE, CG_Q, CG_KV, CG_SUB, CG_FFN, CG_FIN = 0, 8, 24, 27, 29, 30, 38
C_IFD, C_SGD, C_IFM, C_SGM = 46, 47, 48, 49
C_LQ1, C_LK1, C_LQ2, C_LK2 = 50, 114, 178, 242
NCST = 306
WSTAGE = 1024


class _E:
    def __init__(self, name, eng, sem, step):
        self.name, self.eng, self.sem, self.step, self.n = name, eng, sem, step, 0
        self.waited = {}


class _B:
    def __init__(self):
        self.w = None
        self.r = {}


def build(S):
    To = S // 4
    G = S // 512
    Go = To // 512
    NT = S // 128
    nc = bass.Bass("TRN2", target_bir_lowering=False)

    def din(name, shape, dt=F32):
        return nc.dram_tensor(name, list(shape), dt, kind="ExternalInput").ap()

    x_full = din("x_full", [S, D])
    x_own = din("x_own", [To, D])
    posb_full = din("posb_full", [128, S], I32)
    posb_own = din("posb_own", [128, To], I32)
    cst_d = din("cst", [128, NCST])
    ident_d = din("ident", [128, 128])
    maskT_d = din("maskT", [128, 128])
    ipad_d = din("ipad", [128, 32])
    w_dk_d = din("w_dk", [2, 128, 8 * 256])
    w_dkP_d = din("w_dkP", [2, 128, 8 * 256])
    w_dq_d = din("w_dq", [2, 128, 8 * 256])
    w_dqP_d = din("w_dqP", [2, 128, 8 * 256])
    w_dv_d = din("w_dv", [2, 128, 8 * 256])
    w_kvl_d = din("w_kvl", [128, 8 * 256])
    w_kr_d = din("w_kr", [128, 8 * 96])
    w_krP_d = din("w_krP", [128, 8 * 96])
    w_ql_d = din("w_ql", [128, 8 * 384])
    w_ukv_d = din("w_ukv", [128, 2 * 1024])
    w_uq_d = din("w_uq", [128, 3 * 768])
    w_uqP_d = din("w_uqP", [128, 3 * 768])
    w_g_d = din("w_g", [16, 128, 8 * 128])
    w_bm_d = din("w_bm", [8, 128, 4 * 128])
    w_bd_d = din("w_bd", [8, 128, 4 * 128])
    w_o_d = din("w_o", [8, 128, 8 * 128])
    w_fg_d = din("w_fg", [22, 128, 8 * 128])
    w_fu_d = din("w_fu", [22, 128, 8 * 128])
    w_fd_d = din("w_fd", [8, 128, 22 * 128])
    y_d = nc.dram_tensor("y", [To, D], F32, kind="ExternalOutput").ap()

    es = contextlib.ExitStack()
    with es:
        def sem(name):
            return es.enter_context(nc.semaphore(name))

        PE = _E("pe", nc.tensor, sem("s_pe"), 1)
        ACT = _E("act", nc.scalar, sem("s_act"), 1)
        DVE = _E("dve", nc.vector, sem("s_dve"), 1)
        POOL = _E("pool", nc.gpsimd, sem("s_pool"), 1)
        SP = _E("sp", nc.sync, sem("s_sp"), 1)
        NDQ = 40
        DQ = [_E("dq%d" % i, None, sem("s_dq%d" % i), 16) for i in range(NDQ)]
        dq_ctr = [0]

        def _deps(E, reads, writes, pe_acc=False):
            deps = {}
            def add(t):
                if t is None:
                    return
                e2, v = t
                if deps.get(e2.name, (None, 0))[1] < v:
                    deps[e2.name] = (e2, v)
            for b in reads:
                add(b.w)
            for b in writes:
                add(b.w)
                for t in b.r.values():
                    add(t)
            for name, (e2, v) in deps.items():
                if e2 is E and E is PE:
                    continue
                if E.waited.get(name, 0) < v:
                    E.eng.wait_ge(e2.sem, v)
                    E.waited[name] = v

        def _mark(tk, reads, writes):
            for b in reads:
                e2, v = tk
                if b.r.get(e2.name, (None, 0))[1] < v:
                    b.r[e2.name] = tk
            for b in writes:
                b.w = tk
                b.r = {}

        def op(E, fn, reads=(), writes=()):
            _deps(E, reads, writes)
            ins = fn()
            ins.then_inc(E.sem, 1)
            E.n += 1
            _mark((E, E.n), reads, writes)

        def dma(fn, reads=(), writes=()):
            _deps(SP, reads, writes)
            q = DQ[dq_ctr[0] % NDQ]
            dq_ctr[0] += 1
            if q.n > 0 and SP.waited.get(q.name, 0) < q.n:
                nc.sync.wait_ge(q.sem, q.n)
                SP.waited[q.name] = q.n
            ins = fn()
            ins.then_inc(q.sem, 16)
            q.n += 16
            _mark((q, q.n), reads, writes)
            return (q, q.n)

        def barrier():
            srcs = [PE, ACT, DVE, POOL] + [q for q in DQ if q.n > 0]
            for E in (PE, ACT, DVE, POOL, SP):
                for e2 in srcs:
                    if e2.n > 0 and E.waited.get(e2.name, 0) < e2.n:
                        E.eng.wait_ge(e2.sem, e2.n)
                        E.waited[e2.name] = e2.n

        class T:
            def __init__(self, name, shape, dt, psum=False):
                uid[0] += 1
                name = "t%d_%s" % (uid[0], name)
                if psum:
                    self.t = es_cur[0].enter_context(nc.psum_tensor(name, list(shape), dt))
                else:
                    self.t = es_cur[0].enter_context(nc.sbuf_tensor(name, list(shape), dt))
                self.b = _B()

        es_cur = [es]
        uid = [0]

        cst = T("cst", [128, NCST], F32)
        cc = T("cc", [128, 4], F32)
        ident_f = T("ident_f", [128, 128], F32)
        ident_b = T("ident_b", [128, 128], BF16)
        ones_f = T("ones_f", [128, 128], F32)
        ones_b = T("ones_b", [128, 128], BF16)
        maskT_b = T("maskT_b", [128, 128], BF16)
        ipad_b = T("ipad_b", [128, 32], BF16)
        lam = T("lam", [128, 4], F32)
        ctmp = T("ctmp", [128, 128], F32)
        junk = es.enter_context(nc.sbuf_tensor("t_junk", [128, 1024], BF16))
        wst = [T("wst%d" % i, [128, WSTAGE], F32) for i in range(2)]
        wst_ctr = [0]

        dma(lambda: nc.sync.dma_start(out=cst.t[:], in_=cst_d[:]), writes=[cst.b])
        dma(lambda: nc.sync.dma_start(out=ident_f.t[:], in_=ident_d[:]), writes=[ident_f.b])
        op(DVE, lambda: nc.vector.memset(cc.t[:, 0:1], EPS), writes=[cc.b])
        op(DVE, lambda: nc.vector.memset(cc.t[:, 1:2], math.pi / 2.0), writes=[cc.b])
        op(DVE, lambda: nc.vector.memset(ones_f.t[:], 1.0), writes=[ones_f.b])
        op(DVE, lambda: nc.vector.memset(ones_b.t[:], 1.0), writes=[ones_b.b])
        op(DVE, lambda: nc.vector.tensor_copy(out=ident_b.t[:], in_=ident_f.t[:]), reads=[ident_f.b], writes=[ident_b.b])
        dma(lambda: nc.sync.dma_start(out=ctmp.t[:], in_=maskT_d[:]), writes=[ctmp.b])
        op(DVE, lambda: nc.vector.tensor_copy(out=maskT_b.t[:], in_=ctmp.t[:]), reads=[ctmp.b], writes=[maskT_b.b])
        dma(lambda: nc.sync.dma_start(out=ctmp.t[:, 0:32], in_=ipad_d[:]), writes=[ctmp.b])
        op(DVE, lambda: nc.vector.tensor_copy(out=ipad_b.t[:], in_=ctmp.t[:, 0:32]), reads=[ctmp.b], writes=[ipad_b.b])
        for i, (a, b_) in enumerate(((C_LQ1, C_LK1), (C_LQ2, C_LK2))):
            op(DVE, lambda a=a, b_=b_: nc.vector.tensor_tensor(out=ctmp.t[:, 0:64], in0=cst.t[:, a:a + 64], in1=cst.t[:, b_:b_ + 64], op=ALU.mult),
               reads=[cst.b], writes=[ctmp.b])
            op(DVE, lambda i=i: nc.vector.reduce_sum(out=lam.t[:, i:i + 1], in_=ctmp.t[:, 0:64], axis=mybir.AxisListType.X),
               reads=[ctmp.b], writes=[lam.b])
        op(ACT, lambda: nc.scalar.activation(out=lam.t[:, 0:2], in_=lam.t[:, 0:2], func=AF.Exp), reads=[lam.b], writes=[lam.b])
        op(DVE, lambda: nc.vector.tensor_tensor(out=lam.t[:, 2:3], in0=lam.t[:, 1:2], in1=lam.t[:, 0:1], op=ALU.subtract), reads=[lam.b], writes=[lam.b])
        op(DVE, lambda: nc.vector.tensor_scalar(out=lam.t[:, 2:3], in0=lam.t[:, 2:3], scalar1=1.0, scalar2=-0.2, op0=ALU.mult, op1=ALU.add), reads=[lam.b], writes=[lam.b])
        op(DVE, lambda: nc.vector.tensor_scalar(out=lam.t[:, 3:4], in0=cst.t[:, CG_SUB:CG_SUB + 1], scalar1=0.8, scalar2=0.0, op0=ALU.mult, op1=ALU.add), reads=[cst.b, lam.b], writes=[lam.b])

        cast_mode = ["pool"]
        cast_ctr = [0]

        def load_w(dst, dst_ap, src_ap, n):
            off = 0
            while off < n:
                m = min(WSTAGE, n - off)
                st = wst[wst_ctr[0] % len(wst)]
                wst_ctr[0] += 1
                dma(lambda st=st, off=off, m=m: nc.sync.dma_start(out=st.t[:, 0:m], in_=src_ap[:, off:off + m]), writes=[st.b])
                if cast_mode[0] == "pool":
                    op(POOL, lambda st=st, off=off, m=m: nc.gpsimd.tensor_copy(out=dst_ap[:, off:off + m], in_=st.t[:, 0:m]), reads=[st.b], writes=[dst.b])
                else:
                    k = cast_ctr[0] % 2
                    cast_ctr[0] += 1
                    if k == 0:
                        op(ACT, lambda st=st, off=off, m=m: nc.scalar.copy(out=dst_ap[:, off:off + m], in_=st.t[:, 0:m]), reads=[st.b], writes=[dst.b])
                    else:
                        op(DVE, lambda st=st, off=off, m=m: nc.vector.tensor_copy(out=dst_ap[:, off:off + m], in_=st.t[:, 0:m]), reads=[st.b], writes=[dst.b])
                off += m

        o_mla = T("o_mla", [128, 4, To], BF16)
        o_diff = T("o_diff", [128, 4, To], BF16)
        Osb = T("Osb", [128, 512], F32)
        sumsb = T("sumsb", [1, 512], F32)
        o1n = T("o1n", [128, 512], F32)
        dtmp = T("dtmp", [128, 512], F32)
        dtmp2 = T("dtmp2", [128, 512], F32)
        Pb = [T("Pb%d" % i, [128, 512], BF16) for i in range(4)]
        accP = T("accP", [128, 512], F32)
        Ctab = T("Ctab", [128, 512], F32)
        Stab = T("Stab", [128, 512], F32)
        rt1 = dtmp
        rt2 = dtmp2
        recb = dtmp2
        tile_ctr = [0]

        class XPc:
            pass
        XP = XPc()

        def alloc_xp(tables_only=False, nx=3):
            XP.nx = nx
            if not tables_only:
                XP.xin = [T("xin%d" % i, [128, D], F32) for i in range(nx)]
                XP.xs_bf = [T("xs_bf%d" % i, [128, D], BF16) for i in range(2)]
                XP.xnT = [T("xnT%d" % i, [128, 8, 512], BF16) for i in range(2)]
                XP.ssq = [T("ssq%d" % i, [128, 2], F32) for i in range(nx)]
            XP.posi = T("posi", [128, 512], I32)
            XP.tb_a = T("tb_a", [128, 512], F32)
            XP.tb_t = T("tb_t", [128, 512], F32)
            XP.tb_k = T("tb_k", [128, 512], I32)
            XP.tb_kf = T("tb_kf", [128, 512], F32)
            XP.tb_kf2 = T("tb_kf2", [128, 512], F32)

        def make_tables(pos_src, col0, c_if, c_sg, p0, p1, Cout, Sout, ccol0=0):
            posi, tb_a, tb_t, tb_k, tb_kf = XP.posi, XP.tb_a, XP.tb_t, XP.tb_k, XP.tb_kf
            posf = tb_t
            tb_r = tb_kf
            dma(lambda: nc.sync.dma_start(out=posi.t[p0:p1, :], in_=pos_src[p0:p1, col0:col0 + 512]), writes=[posi.b])
            op(DVE, lambda: nc.vector.tensor_copy(out=posf.t[p0:p1, :], in_=posi.t[p0:p1, :]), reads=[posi.b], writes=[posf.b])
            op(DVE, lambda: nc.vector.tensor_scalar_mul(tb_a.t[p0:p1, :], posf.t[p0:p1, :], cst.t[p0:p1, c_if:c_if + 1]),
               reads=[posf.b, cst.b], writes=[tb_a.b])
            for which in (0, 1):
                tb_kf = XP.tb_kf if which == 0 else XP.tb_kf2
                tb_r = tb_kf
                sh = 0.0 if which == 0 else 0.25
                op(DVE, lambda sh=sh: nc.vector.tensor_scalar(out=tb_t.t[p0:p1, :], in0=tb_a.t[p0:p1, :], scalar1=INV2PI, scalar2=sh, op0=ALU.mult, op1=ALU.add),
                   reads=[tb_a.b], writes=[tb_t.b])
                op(DVE, lambda: nc.vector.tensor_copy(out=tb_k.t[p0:p1, :], in_=tb_t.t[p0:p1, :]), reads=[tb_t.b], writes=[tb_k.b])
                op(DVE, lambda: nc.vector.tensor_copy(out=tb_kf.t[p0:p1, :], in_=tb_k.t[p0:p1, :]), reads=[tb_k.b], writes=[tb_kf.b])
                op(DVE, lambda: nc.vector.scalar_tensor_tensor(out=tb_t.t[p0:p1, :], in0=tb_kf.t[p0:p1, :], scalar=-C1, in1=tb_a.t[p0:p1, :], op0=ALU.mult, op1=ALU.add),
                   reads=[tb_kf.b, tb_a.b], writes=[tb_t.b])
                op(DVE, lambda: nc.vector.scalar_tensor_tensor(out=tb_r.t[p0:p1, :], in0=tb_kf.t[p0:p1, :], scalar=-C2, in1=tb_t.t[p0:p1, :], op0=ALU.mult, op1=ALU.add),
                   reads=[tb_kf.b, tb_t.b], writes=[tb_r.b])
                lo = -PI_LO if which == 0 else -PI_LO - math.pi / 2.0
                hi = PI_LO if which == 0 else PI_LO - math.pi / 2.0
                op(DVE, lambda lo=lo, hi=hi: nc.vector.tensor_scalar(out=tb_r.t[p0:p1, :], in0=tb_r.t[p0:p1, :], scalar1=hi, scalar2=lo, op0=ALU.min, op1=ALU.max),
                   reads=[tb_r.b], writes=[tb_r.b])
                if which == 0:
                    op(ACT, lambda: nc.scalar.activation(out=Sout.t[p0:p1, ccol0:ccol0 + 512], in_=tb_r.t[p0:p1, :], func=AF.Sin, scale=cst.t[p0:p1, c_sg:c_sg + 1]),
                       reads=[tb_r.b, cst.b], writes=[Sout.b])
                else:
                    op(ACT, lambda: nc.scalar.activation(out=Cout.t[p0:p1, ccol0:ccol0 + 512], in_=tb_r.t[p0:p1, :], func=AF.Sin, bias=cc.t[p0:p1, 1:2]),
                       reads=[tb_r.b, cc.b], writes=[Cout.b])

        def xpass_group(xsrc, g, pt, gi, per_tile=None):
            xo = XP.xnT[gi % 2]
            xin, ssq, xs_bf = XP.xin, XP.ssq, XP.xs_bf
            for j in range(4):
                k = tile_ctr[0]
                tile_ctr[0] += 1
                xi = xin[k % XP.nx]
                sq = ssq[k % XP.nx]
                xs = xs_bf[k % 2]
                r0 = g * 512 + j * 128
                dma(lambda xi=xi, r0=r0: nc.sync.dma_start(out=xi.t[:], in_=xsrc[r0:r0 + 128, :]), writes=[xi.b])
                op(ACT, lambda xi=xi, sq=sq: nc.scalar.activation(out=junk[:, :], in_=xi.t[:], func=AF.Square, accum_out=sq.t[:, 0:1]),
                   reads=[xi.b], writes=[sq.b])
                op(ACT, lambda sq=sq: nc.scalar.activation(out=sq.t[:, 1:2], in_=sq.t[:, 0:1], func=AF.Sqrt, scale=1.0 / D, bias=cc.t[:, 0:1]),
                   reads=[sq.b, cc.b], writes=[sq.b])
                op(DVE, lambda sq=sq: nc.vector.reciprocal(out=sq.t[:, 1:2], in_=sq.t[:, 1:2]), reads=[sq.b], writes=[sq.b])
                op(DVE, lambda xi=xi, sq=sq, xs=xs: nc.vector.tensor_scalar_mul(xs.t[:], xi.t[:], sq.t[:, 1:2]),
                   reads=[xi.b, sq.b], writes=[xs.b])
                for c in range(8):
                    p = pt[c // 4]
                    op(PE, lambda c=c, j=j, p=p, xs=xs: nc.tensor.transpose(out=p.t[:, c % 4, j * 128:(j + 1) * 128], in_=xs.t[:, c * 128:(c + 1) * 128], identity=ident_b.t[:]),
                       reads=[xs.b, ident_b.b], writes=[p.b])
                if per_tile is not None:
                    per_tile(j, xi)
            for c in range(8):
                p = pt[c // 4]
                if c % 2 == 0:
                    op(ACT, lambda c=c, p=p: nc.scalar.mul(out=xo.t[:, c, :], in_=p.t[:, c % 4, :], mul=cst.t[:, CG_MIX + c:CG_MIX + c + 1]),
                       reads=[p.b, cst.b], writes=[xo.b])
                else:
                    op(DVE, lambda c=c, p=p: nc.vector.tensor_scalar_mul(xo.t[:, c, :], p.t[:, c % 4, :], cst.t[:, CG_MIX + c:CG_MIX + c + 1]),
                       reads=[p.b, cst.b], writes=[xo.b])
            return xo

        gi_glob = [0]

        def pipelined(xsrc, n, pt, tables_fn, stage2_fn):
            xs = [None] * n
            xs[0] = xpass_group(xsrc, 0, pt, gi_glob[0])
            gi_glob[0] += 1
            for g in range(n):
                if g + 1 < n:
                    xs[g + 1] = xpass_group(xsrc, g + 1, pt, gi_glob[0])
                    gi_glob[0] += 1
                if tables_fn is not None:
                    tables_fn(g)
                stage2_fn(g, xs[g])

        def proj_fm(ps, xo, w, wcols, col0, m, nk=8):
            for c in range(nk):
                op(PE, lambda c=c: nc.tensor.matmul(ps.t[0:m, :], lhsT=w.t[:, c, col0:col0 + m], rhs=xo.t[:, c, :], start=(c == 0), stop=(c == nk - 1)),
                   reads=[w.b, xo.b], writes=[ps.b])

        def rope_combine(psA, psB, Ct, St, p0, p1, out_t, out_ap, tcol0=0, out_aps=None):
            op(DVE, lambda: nc.vector.tensor_tensor(out=rt1.t[p0:p1, :], in0=psA.t[p0:p1, :], in1=Ct.t[p0:p1, tcol0:tcol0 + 512], op=ALU.mult),
               reads=[psA.b, Ct.b], writes=[rt1.b])
            op(DVE, lambda: nc.vector.tensor_tensor(out=rt2.t[p0:p1, :], in0=psB.t[p0:p1, :], in1=St.t[p0:p1, tcol0:tcol0 + 512], op=ALU.mult),
               reads=[psB.b, St.b], writes=[rt2.b])
            if out_aps is None:
                out_aps = [(p0, p1, out_ap)]
            for (a0, a1, oap) in out_aps:
                op(POOL, lambda a0=a0, a1=a1, oap=oap: nc.gpsimd.tensor_tensor(out=oap, in0=rt1.t[a0:a1, :], in1=rt2.t[a0:a1, :], op=ALU.add),
                   reads=[rt1.b, rt2.b], writes=[out_t.b])

        def attention(KT, kt_ap, QT, q_ap, Vt, v_ap, dkrows, dvp, sumsep, s, ps_S, ps_O, ps_sum, scale, mb=0, ps_bc=None, mpad=None):
            units = []
            for kt in range(16 * s + 16):
                t = kt - 16 * s
                units.append((kt, 32 * t if t > 0 else 0, t >= 0))
            n = len(units)

            def QK(u):
                kt, c0, diag = units[u]
                S_ = ps_S[u % len(ps_S)]
                op(PE, lambda: nc.tensor.matmul(S_.t[:, c0:512], lhsT=kt_ap(kt), rhs=q_ap(c0), start=True, stop=(not diag)),
                   reads=[KT.b, QT.b], writes=[S_.b])
                if diag:
                    op(PE, lambda: nc.tensor.matmul(S_.t[:, c0:c0 + 32], lhsT=maskT_b.t[mb:mb + dkrows, :], rhs=ipad_b.t[mb:mb + dkrows, :], start=False, stop=True),
                       reads=[maskT_b.b, ipad_b.b], writes=[S_.b])

            def EXP(u):
                kt, c0, diag = units[u]
                S_ = ps_S[u % len(ps_S)]
                P_ = Pb[u % 4]
                op(ACT, lambda: nc.scalar.activation(out=P_.t[:, c0:512], in_=S_.t[:, c0:512], func=AF.Exp, scale=scale),
                   reads=[S_.b], writes=[P_.b])

            def PV(u):
                kt, c0, diag = units[u]
                P_ = Pb[u % 4]
                mo = mpad if mpad is not None else dvp
                op(PE, lambda: nc.tensor.matmul(ps_O.t[0:mo, c0:512], lhsT=v_ap(kt), rhs=P_.t[:, c0:512], start=(u == 0), stop=(u == n - 1)),
                   reads=[Vt.b, P_.b], writes=[ps_O.b])
                if sumsep:
                    if u % 2 == 1:
                        op(DVE, lambda: nc.vector.tensor_tensor(out=accP.t[:, c0:512], in0=accP.t[:, c0:512], in1=P_.t[:, c0:512], op=ALU.add),
                           reads=[P_.b, accP.b], writes=[accP.b])
                    else:
                        op(PE, lambda: nc.tensor.matmul(ps_sum.t[:, c0:512], lhsT=ones_b.t[:, :], rhs=P_.t[:, c0:512], start=(u == 0), stop=False),
                           reads=[ones_b.b, P_.b], writes=[ps_sum.b])

            if sumsep:
                op(POOL, lambda: nc.gpsimd.memset(accP.t[:, :], 0.0), writes=[accP.b])
            LA = len(ps_S) - 1
            for u in range(min(LA, n)):
                QK(u)
            EXP(0)
            for u in range(n):
                if u + LA < n:
                    QK(u + LA)
                if u + 1 < n:
                    EXP(u + 1)
                PV(u)
            nd = dvp if sumsep else dvp - 1
            op(DVE, lambda: nc.vector.tensor_copy(out=Osb.t[0:nd, :], in_=ps_O.t[0:nd, :]), reads=[ps_O.b], writes=[Osb.b])
            bc = ps_bc if ps_bc is not None else ps_S[0]
            if sumsep:
                op(PE, lambda: nc.tensor.matmul(ps_sum.t[:, :], lhsT=ones_f.t[:, :], rhs=accP.t[:, :], start=False, stop=True),
                   reads=[ones_f.b, accP.b], writes=[ps_sum.b])
                op(DVE, lambda: nc.vector.reciprocal(out=recb.t[:, :], in_=ps_sum.t[:, :]), reads=[ps_sum.b], writes=[recb.b])
                return recb, nd
            op(DVE, lambda: nc.vector.tensor_copy(out=sumsb.t[0:1, :], in_=ps_O.t[nd:nd + 1, :]), reads=[ps_O.b], writes=[sumsb.b])
            op(DVE, lambda: nc.vector.reciprocal(out=sumsb.t[0:1, :], in_=sumsb.t[0:1, :]), reads=[sumsb.b], writes=[sumsb.b])
            op(PE, lambda: nc.tensor.matmul(bc.t[0:nd, :], lhsT=ones_f.t[0:1, 0:nd], rhs=sumsb.t[0:1, :], start=True, stop=True),
               reads=[ones_f.b, sumsb.b], writes=[bc.b])
            return bc, nd

        for hp in range(2):
            ph = contextlib.ExitStack()
            barrier()
            es_cur[0] = ph
            with ph:
                KTd = T("KTd", [128, 2, S], BF16)
                Vd = T("Vd", [128, NT, 256], BF16)
                QTd = T("QTd", [128, 2, 2, To], BF16)
                wA = T("wA", [128, 8, 256], BF16)
                wB = T("wB", [128, 8, 256], BF16)
                wV = T("wV", [128, 8, 256], BF16)
                px = contextlib.ExitStack()
                es_cur[0] = px
                pt = [T("pt%d" % i, [128, 4, 512], BF16, psum=True) for i in range(2)]
                pj = [T("pj%d" % i, [128, 512], F32, psum=True) for i in range(4)]
                alloc_xp(nx=4)
                load_w(wA, wA.t[:].rearrange("p c n -> p (c n)"), w_dq_d[hp], 8 * 256)
                load_w(wB, wB.t[:].rearrange("p c n -> p (c n)"), w_dqP_d[hp], 8 * 256)
                op(POOL, lambda: nc.gpsimd.memset(QTd.t[:].rearrange("p a b n -> p (a b n)"), 0.0), writes=[QTd.b])

                def q_stage2(g, xo):
                    for hh in range(2):
                        proj_fm(pj[0], xo, wA, 256, hh * 128, 128)
                        proj_fm(pj[1], xo, wB, 256, hh * 128, 128)
                        rope_combine(pj[0], pj[1], Ctab, Stab, 0, 128, QTd, None,
                                     out_aps=[(0, 64, QTd.t[0:64, hh, 0, g * 512:(g + 1) * 512]), (64, 128, QTd.t[64:128, hh, 1, g * 512:(g + 1) * 512])])
                pipelined(x_own, Go, pt, lambda g: make_tables(posb_own, g * 512, C_IFD, C_SGD, 0, 128, Ctab, Stab), q_stage2)
                load_w(wA, wA.t[:].rearrange("p c n -> p (c n)"), w_dk_d[hp], 8 * 256)
                load_w(wB, wB.t[:].rearrange("p c n -> p (c n)"), w_dkP_d[hp], 8 * 256)
                load_w(wV, wV.t[:].rearrange("p c n -> p (c n)"), w_dv_d[hp], 8 * 256)

                def k_stage2(g, xo):
                    for hh in range(2):
                        proj_fm(pj[0], xo, wA, 256, hh * 128, 128)
                        proj_fm(pj[1], xo, wB, 256, hh * 128, 128)
                        rope_combine(pj[0], pj[1], Ctab, Stab, 0, 128, KTd, KTd.t[:, hh, g * 512:(g + 1) * 512])
                    for j in range(4):
                        pv = pj[2 + (j % 2)]
                        for c in range(8):
                            op(PE, lambda c=c, j=j, pv=pv: nc.tensor.matmul(pv.t[:, 0:256], lhsT=xo.t[:, c, j * 128:(j + 1) * 128], rhs=wV.t[:, c, :], start=(c == 0), stop=(c == 7)),
                               reads=[xo.b, wV.b], writes=[pv.b])
                        op(ACT, lambda j=j, pv=pv, g=g: nc.scalar.copy(out=Vd.t[:, g * 4 + j, :], in_=pv.t[:, 0:256]), reads=[pv.b], writes=[Vd.b])
                pipelined(x_full, G, pt, lambda g: make_tables(posb_full, g * 512, C_IFD, C_SGD, 0, 128, Ctab, Stab), k_stage2)
                barrier()
                px.close()
                py = contextlib.ExitStack()
                es_cur[0] = py
                ps_S = [T("dS%d" % i, [128, 512], F32, psum=True) for i in range(3)]
                ps_Od = [T("dO%d" % i, [128, 512], F32, psum=True) for i in range(2)]
                ps_sumd = T("dsum", [128, 512], F32, psum=True)
                psL = T("dL", [128, 512], F32, psum=True)
                dseq = [0]
                for hh in range(2):
                    h = hp * 2 + hh
                    for s in range(Go):
                        for comp in range(2):
                            b0 = comp * 64
                            ps_O_ = ps_Od[dseq[0] % 2]
                            dseq[0] += 1
                            bc, nd = attention(
                                KTd, lambda kt: KTd.t[:, hh, kt * 128:(kt + 1) * 128],
                                QTd, lambda c0: QTd.t[:, hh, comp, s * 512 + c0:(s + 1) * 512],
                                Vd, lambda kt: Vd.t[:, kt, hh * 128:(hh + 1) * 128],
                                128, 128, True, s, ps_S, ps_O_, ps_sumd, 0.125, mb=0)
                            if comp == 0:
                                op(DVE, lambda: nc.vector.tensor_tensor(out=o1n.t[:], in0=Osb.t[:], in1=bc.t[:], op=ALU.mult),
                                   reads=[Osb.b, bc.b], writes=[o1n.b])
                            else:
                                op(DVE, lambda: nc.vector.tensor_tensor(out=dtmp.t[:], in0=Osb.t[:], in1=bc.t[:], op=ALU.mult),
                                   reads=[Osb.b, bc.b], writes=[dtmp.b])
                                op(DVE, lambda: nc.vector.scalar_tensor_tensor(out=dtmp.t[:], in0=dtmp.t[:], scalar=lam.t[:, 2:3], in1=o1n.t[:], op0=ALU.mult, op1=ALU.add),
                                   reads=[dtmp.b, lam.b, o1n.b], writes=[dtmp.b])
                                op(DVE, lambda: nc.vector.tensor_tensor(out=dtmp2.t[:], in0=dtmp.t[:], in1=dtmp.t[:], op=ALU.mult),
                                   reads=[dtmp.b], writes=[dtmp2.b])
                                op(PE, lambda: nc.tensor.matmul(psL.t[:], lhsT=ones_f.t[:], rhs=dtmp2.t[:], start=True, stop=True),
                                   reads=[ones_f.b, dtmp2.b], writes=[psL.b])
                                op(ACT, lambda: nc.scalar.activation(out=dtmp2.t[:], in_=psL.t[:], func=AF.Sqrt, scale=1.0 / 128, bias=cc.t[:, 0:1]),
                                   reads=[psL.b, cc.b], writes=[dtmp2.b])
                                op(DVE, lambda: nc.vector.reciprocal(out=dtmp2.t[:], in_=dtmp2.t[:]), reads=[dtmp2.b], writes=[dtmp2.b])
                                op(DVE, lambda: nc.vector.scalar_tensor_tensor(out=o_diff.t[:, h, s * 512:(s + 1) * 512], in0=dtmp.t[:], scalar=lam.t[:, 3:4], in1=dtmp2.t[:], op0=ALU.mult, op1=ALU.mult),
                                   reads=[dtmp.b, lam.b, dtmp2.b], writes=[o_diff.b])
                barrier()
                py.close()
                es_cur[0] = ph
            barrier()
        es_cur[0] = es

        ph = contextlib.ExitStack()
        es_cur[0] = ph
        with ph:
            kvnT = T("kvnT", [128, 2, S], BF16)
            KTm = [T("KTm%d" % i, [96, S], BF16) for i in range(2)]
            qnT = T("qnT", [128, 3, To], BF16)
            warena = T("warena", [128, 6656], BF16)

            class _V:
                def __init__(self, off, c, n):
                    self.t = warena.t[:, off:off + c * n].rearrange("p (c n) -> p c n", c=c)
                    self.flat = warena.t[:, off:off + c * n]
                    self.b = warena.b
            w_ql = _V(0, 8, 384)
            w_kvl = _V(0, 8, 256)
            w_kr = _V(2048, 8, 96)
            w_krP = _V(2048 + 768, 8, 96)
            w_ukv = _V(0, 2, 1024)
            w_uq = _V(2048, 3, 768)
            w_uqP = _V(2048 + 2304, 3, 768)

            pa = contextlib.ExitStack()
            es_cur[0] = pa
            with pa:
                alloc_xp()
                lat = T("lat", [128, 3, 512], F32)
                latsq = T("latsq", [128, 512], BF16)
                rsb = Osb
                pt = [T("ptm%d" % i, [128, 4, 512], BF16, psum=True) for i in range(2)]
                pj = [T("pjm%d" % i, [128, 512], F32, psum=True) for i in range(4)]

                def latent_norm(xo, w, nch, gcol, out_t, out_col0):
                    for ch in range(nch):
                        p = pj[ch % 2]
                        proj_fm(p, xo, w, nch * 128, ch * 128, 128)
                        op(ACT, lambda ch=ch, p=p: nc.scalar.copy(out=lat.t[:, ch, :], in_=p.t[:]), reads=[p.b], writes=[lat.b])
                        op(DVE, lambda ch=ch: nc.vector.tensor_tensor(out=latsq.t[:], in0=lat.t[:, ch, :], in1=lat.t[:, ch, :], op=ALU.mult),
                           reads=[lat.b], writes=[latsq.b])
                        op(PE, lambda ch=ch: nc.tensor.matmul(pj[2].t[:], lhsT=ones_b.t[:], rhs=latsq.t[:], start=(ch == 0), stop=(ch == nch - 1)),
                           reads=[ones_b.b, latsq.b], writes=[pj[2].b])
                    op(ACT, lambda: nc.scalar.activation(out=rsb.t[:], in_=pj[2].t[:], func=AF.Sqrt, scale=1.0 / (nch * 128), bias=cc.t[:, 0:1]),
                       reads=[pj[2].b, cc.b], writes=[rsb.b])
                    op(DVE, lambda: nc.vector.reciprocal(out=rsb.t[:], in_=rsb.t[:]), reads=[rsb.b], writes=[rsb.b])
                    for ch in range(nch):
                        op(DVE, lambda ch=ch: nc.vector.scalar_tensor_tensor(out=out_t.t[:, ch, out_col0:out_col0 + 512], in0=lat.t[:, ch, :], scalar=cst.t[:, gcol + ch:gcol + ch + 1], in1=rsb.t[:], op0=ALU.mult, op1=ALU.mult),
                           reads=[lat.b, cst.b, rsb.b], writes=[out_t.b])

                load_w(w_ql, w_ql.flat, w_ql_d, 8 * 384)
                pipelined(x_own, Go, pt, None, lambda g, xo: latent_norm(xo, w_ql, 3, CG_Q, qnT, g * 512))
                load_w(w_kvl, w_kvl.flat, w_kvl_d, 8 * 256)
                load_w(w_kr, w_kr.flat, w_kr_d, 8 * 96)
                load_w(w_krP, w_krP.flat, w_krP_d, 8 * 96)

                def mk_stage2(g, xo):
                    latent_norm(xo, w_kvl, 2, CG_KV, kvnT, g * 512)
                    proj_fm(pj[0], xo, w_kr, 96, 0, 96)
                    proj_fm(pj[1], xo, w_krP, 96, 0, 96)
                    rope_combine(pj[0], pj[1], Ctab, Stab, 64, 96, KTm[0], KTm[0].t[64:96, g * 512:(g + 1) * 512])
                    op(POOL, lambda g=g: nc.gpsimd.tensor_copy(out=KTm[1].t[64:96, g * 512:(g + 1) * 512], in_=KTm[0].t[64:96, g * 512:(g + 1) * 512]),
                       reads=[KTm[0].b], writes=[KTm[1].b])
                pipelined(x_full, G, pt, lambda g: make_tables(posb_full, g * 512, C_IFM, C_SGM, 64, 96, Ctab, Stab), mk_stage2)
            barrier()
            es_cur[0] = ph

            pb = contextlib.ExitStack()
            es_cur[0] = pb
            with pb:
                alloc_xp(tables_only=True)
                Vm = [T("Vm%d" % i, [128, NT * 65 + 64], BF16) for i in range(2)]
                QTm = [T("QTm%d" % i, [96, To], BF16) for i in range(2)]
                CqT = T("CqT", [96, To], F32)
                SqT = T("SqT", [96, To], F32)
                ps_S = [T("psS%d" % i, [128, 512], F32, psum=True) for i in range(3)]
                ps_Os = [T("psO%d" % i, [128, 512], F32, psum=True) for i in range(2)]
                ps_bc = T("psbc", [128, 512], F32, psum=True)
                pbk = [T("pbk%d" % i, [128, 512], F32, psum=True) for i in range(2)]
                seq_ctr = [0]
                load_w(w_ukv, w_ukv.flat, w_ukv_d, 2 * 1024)
                load_w(w_uq, w_uq.flat, w_uq_d, 3 * 768)
                load_w(w_uqP, w_uqP.flat, w_uqP_d, 3 * 768)
                for i in range(2):
                    op(DVE, lambda i=i: nc.vector.memset(Vm[i].t[:, :], 0.0), writes=[Vm[i].b])
                    op(DVE, lambda i=i: nc.vector.memset(Vm[i].t[:, 0:NT * 65].rearrange("p (a b) -> p a b", b=65)[:, :, 64:65], 1.0), writes=[Vm[i].b])
                for g in range(Go):
                    make_tables(posb_own, g * 512, C_IFM, C_SGM, 64, 96, CqT, SqT, ccol0=g * 512)
                bk_ctr = [0]

                def nextbank():
                    p = pbk[bk_ctr[0] % 2]
                    bk_ctr[0] += 1
                    return p

                def build_head(h):
                    KT_, V_, Q_ = KTm[h % 2], Vm[h % 2], QTm[h % 2]
                    for g in range(Go):
                        pA = nextbank()
                        pB = nextbank()
                        for ch in range(3):
                            op(PE, lambda ch=ch, g=g: nc.tensor.matmul(pA.t[0:96, :], lhsT=w_uq.t[:, ch, h * 96:(h + 1) * 96], rhs=qnT.t[:, ch, g * 512:(g + 1) * 512], start=(ch == 0), stop=(ch == 2)),
                               reads=[w_uq.b, qnT.b], writes=[pA.b])
                        for ch in range(3):
                            op(PE, lambda ch=ch, g=g: nc.tensor.matmul(pB.t[0:96, :], lhsT=w_uqP.t[:, ch, h * 96:(h + 1) * 96], rhs=qnT.t[:, ch, g * 512:(g + 1) * 512], start=(ch == 0), stop=(ch == 2)),
                               reads=[w_uqP.b, qnT.b], writes=[pB.b])
                        op(DVE, lambda g=g: nc.vector.tensor_copy(out=Q_.t[0:64, g * 512:(g + 1) * 512], in_=pA.t[0:64, :]), reads=[pA.b], writes=[Q_.b])
                        rope_combine(pA, pB, CqT, SqT, 64, 96, Q_, Q_.t[64:96, g * 512:(g + 1) * 512], tcol0=g * 512)
                    for g in range(G):
                        p = nextbank()
                        for ch in range(2):
                            op(PE, lambda ch=ch, g=g, p=p: nc.tensor.matmul(p.t[0:64, :], lhsT=w_ukv.t[:, ch, h * 128:h * 128 + 64], rhs=kvnT.t[:, ch, g * 512:(g + 1) * 512], start=(ch == 0), stop=(ch == 1)),
                               reads=[w_ukv.b, kvnT.b], writes=[p.b])
                        op(DVE, lambda g=g, p=p: nc.vector.tensor_copy(out=KT_.t[0:64, g * 512:(g + 1) * 512], in_=p.t[0:64, :]), reads=[p.b], writes=[KT_.b])
                    for t8 in range(NT // 8):
                        p = nextbank()
                        for tt in range(8):
                            tok = (t8 * 8 + tt) * 128
                            for ch in range(2):
                                op(PE, lambda ch=ch, tt=tt, tok=tok, p=p: nc.tensor.matmul(p.t[:, tt * 64:(tt + 1) * 64], lhsT=kvnT.t[:, ch, tok:tok + 128], rhs=w_ukv.t[:, ch, h * 128 + 64:h * 128 + 128], start=(ch == 0), stop=(ch == 1)),
                                   reads=[kvnT.b, w_ukv.b], writes=[p.b])
                        op(DVE, lambda t8=t8, p=p: nc.vector.tensor_copy(out=V_.t[:, t8 * 8 * 65:(t8 + 1) * 8 * 65].rearrange("p (a b) -> p a b", b=65)[:, :, 0:64], in_=p.t[:].rearrange("p (a b) -> p a b", a=8)), reads=[p.b], writes=[V_.b])

                build_head(0)
                for h in range(8):
                    if h + 1 < 8:
                        build_head(h + 1)
                    KT_, V_, Q_ = KTm[h % 2], Vm[h % 2], QTm[h % 2]
                    for s in range(Go):
                        ps_O = ps_Os[seq_ctr[0] % 2]
                        seq_ctr[0] += 1
                        bc, nd = attention(
                            KT_, lambda kt: KT_.t[0:96, kt * 128:(kt + 1) * 128],
                            Q_, lambda c0: Q_.t[0:96, s * 512 + c0:(s + 1) * 512],
                            V_, lambda kt: V_.t[:, kt * 65:kt * 65 + 128],
                            96, 65, False, s, ps_S, ps_O, None, 96.0 ** -0.5, ps_bc=ps_bc, mpad=128)
                        po = (h % 2) * 64
                        op(DVE, lambda: nc.vector.tensor_tensor(out=o_mla.t[po:po + 64, h // 2, s * 512:(s + 1) * 512], in0=Osb.t[0:64, :], in1=bc.t[0:64, :], op=ALU.mult),
                           reads=[Osb.b, bc.b], writes=[o_mla.b])
            barrier()
            es_cur[0] = ph
        barrier()
        es_cur[0] = es

        ph = contextlib.ExitStack()
        es_cur[0] = ph
        with ph:
            NSLOT = 5
            slots = [T("wsl%d" % i, [128, 22 * 128], BF16) for i in range(NSLOT)]
            sl_ctr = [0]
            xT = T("xT", [128, 8, 512], F32)
            sig = T("sig", [128, 2, 512], BF16)
            mt1 = dtmp
            mt2 = dtmp2
            merged = T("merged", [128, 8, 512], BF16)
            hn = T("hn", [128, 8, 512], BF16)
            hid = T("hid", [128, 22, 512], BF16)
            sqx = Pb[0]
            rsx = T("rsx", [128, 512], F32)
            sg = Osb
            ysb = T("ysb", [128, D], F32)
            pt = [T("ptt%d" % i, [128, 4, 512], BF16, psum=True) for i in range(2)]
            pj = [T("pjt%d" % i, [128, 512], F32, psum=True) for i in range(4)]

            def wblock(src_ap, n):
                sl = slots[sl_ctr[0] % NSLOT]
                sl_ctr[0] += 1
                load_w(sl, sl.t[:, 0:n], src_ap, n)
                return sl

            def rms_bcast(src):
                for c in range(8):
                    op(DVE, lambda c=c: nc.vector.tensor_tensor(out=sqx.t[:], in0=src.t[:, c, :], in1=src.t[:, c, :], op=ALU.mult),
                       reads=[src.b], writes=[sqx.b])
                    op(PE, lambda c=c: nc.tensor.matmul(pj[3].t[:], lhsT=ones_b.t[:], rhs=sqx.t[:], start=(c == 0), stop=(c == 7)),
                       reads=[ones_b.b, sqx.b], writes=[pj[3].b])
                op(ACT, lambda: nc.scalar.activation(out=rsx.t[:], in_=pj[3].t[:], func=AF.Sqrt, scale=1.0 / D, bias=cc.t[:, 0:1]),
                   reads=[pj[3].b, cc.b], writes=[rsx.b])
                op(DVE, lambda: nc.vector.reciprocal(out=rsx.t[:], in_=rsx.t[:]), reads=[rsx.b], writes=[rsx.b])

            alloc_xp()
            cast_mode[0] = "actdve"
            wst.extend([T("wstx%d" % i, [128, WSTAGE], F32) for i in range(2)])
            jobs = []
            gi_box = [0]

            def add_group(g):
                tsl = slice(g * 512, (g + 1) * 512)
                st = {}

                def pre(_):
                    def xt_tile(j, xi):
                        for half in range(2):
                            p = pj[half]
                            for c4 in range(4):
                                c = half * 4 + c4
                                op(PE, lambda c=c, c4=c4, p=p: nc.tensor.transpose(out=p.t[:, c4 * 128:(c4 + 1) * 128], in_=xi.t[:, c * 128:(c + 1) * 128], identity=ident_f.t[:]),
                                   reads=[xi.b, ident_f.b], writes=[p.b])
                            op(DVE, lambda half=half, p=p: nc.vector.tensor_copy(out=xT.t[:, half * 4:(half + 1) * 4, j * 128:(j + 1) * 128], in_=p.t[:].rearrange("p (a b) -> p a b", a=4)),
                               reads=[p.b], writes=[xT.b])
                    st["xo"] = xpass_group(x_own, g, pt, gi_glob[0], per_tile=xt_tile)
                    gi_glob[0] += 1
                jobs.append((None, 0, pre))

                def gate(br, e):
                    def fn(wg):
                        xo = st["xo"]
                        p = pj[br]
                        for c in range(8):
                            op(PE, lambda c=c: nc.tensor.matmul(p.t[:], lhsT=wg.t[:, c * 128:(c + 1) * 128], rhs=xo.t[:, c, :], start=(c == 0), stop=(c == 7)),
                               reads=[wg.b, xo.b], writes=[p.b])
                        bcol = CB_GATE + br * 8 + e
                        op(ACT, lambda: nc.scalar.activation(out=sig.t[:, br, :], in_=p.t[:], func=AF.Sigmoid, bias=cst.t[:, bcol:bcol + 1]),
                           reads=[p.b, cst.b], writes=[sig.b])
                    return fn

                def bm(e):
                    def fn(wm):
                        for c in range(4):
                            op(PE, lambda c=c: nc.tensor.matmul(pj[2].t[:], lhsT=wm.t[:, c * 128:(c + 1) * 128], rhs=o_mla.t[:, c, tsl], start=(c == 0), stop=(c == 3)),
                               reads=[wm.b, o_mla.b], writes=[pj[2].b])
                    return fn

                def bd(e):
                    def fn(wd):
                        for c in range(4):
                            op(PE, lambda c=c: nc.tensor.matmul(pj[3].t[:], lhsT=wd.t[:, c * 128:(c + 1) * 128], rhs=o_diff.t[:, c, tsl], start=(c == 0), stop=(c == 3)),
                               reads=[wd.b, o_diff.b], writes=[pj[3].b])
                        op(DVE, lambda: nc.vector.tensor_tensor(out=mt1.t[:], in0=pj[2].t[:], in1=sig.t[:, 0, :], op=ALU.mult), reads=[pj[2].b, sig.b], writes=[mt1.b])
                        op(DVE, lambda: nc.vector.tensor_tensor(out=mt2.t[:], in0=pj[3].t[:], in1=sig.t[:, 1, :], op=ALU.mult), reads=[pj[3].b, sig.b], writes=[mt2.b])
                        op(POOL, lambda: nc.gpsimd.tensor_tensor(out=merged.t[:, e, :], in0=mt1.t[:], in1=mt2.t[:], op=ALU.add), reads=[mt1.b, mt2.b], writes=[merged.b])
                    return fn

                for e in range(8):
                    jobs.append((w_g_d[e], 8 * 128, gate(0, e)))
                    jobs.append((w_g_d[8 + e], 8 * 128, gate(1, e)))
                    jobs.append((w_bm_d[e], 4 * 128, bm(e)))
                    jobs.append((w_bd_d[e], 4 * 128, bd(e)))

                def outp(e):
                    def fn(wo):
                        p = pj[e % 2]
                        for c in range(8):
                            op(PE, lambda c=c: nc.tensor.matmul(p.t[:], lhsT=wo.t[:, c * 128:(c + 1) * 128], rhs=merged.t[:, c, :], start=(c == 0), stop=(c == 7)),
                               reads=[wo.b, merged.b], writes=[p.b])
                        op(DVE, lambda: nc.vector.tensor_tensor(out=xT.t[:, e, :], in0=xT.t[:, e, :], in1=p.t[:], op=ALU.add), reads=[xT.b, p.b], writes=[xT.b])
                    return fn
                for e in range(8):
                    jobs.append((w_o_d[e], 8 * 128, outp(e)))

                def ffn_norm(_):
                    rms_bcast(xT)
                    for c in range(8):
                        op(DVE, lambda c=c: nc.vector.scalar_tensor_tensor(out=hn.t[:, c, :], in0=xT.t[:, c, :], scalar=cst.t[:, CG_FFN + c:CG_FFN + c + 1], in1=rsx.t[:], op0=ALU.mult, op1=ALU.mult),
                           reads=[xT.b, cst.b, rsx.b], writes=[hn.b])
                jobs.append((None, 0, ffn_norm))

                def fgate(f):
                    def fn(w):
                        for c in range(8):
                            op(PE, lambda c=c: nc.tensor.matmul(pj[0].t[:], lhsT=w.t[:, c * 128:(c + 1) * 128], rhs=hn.t[:, c, :], start=(c == 0), stop=(c == 7)),
                               reads=[w.b, hn.b], writes=[pj[0].b])
                        op(ACT, lambda: nc.scalar.activation(out=sg.t[:], in_=pj[0].t[:], func=AF.Silu), reads=[pj[0].b], writes=[sg.b])
                    return fn

                def fup(f):
                    def fn(w):
                        for c in range(8):
                            op(PE, lambda c=c: nc.tensor.matmul(pj[1].t[:], lhsT=w.t[:, c * 128:(c + 1) * 128], rhs=hn.t[:, c, :], start=(c == 0), stop=(c == 7)),
                               reads=[w.b, hn.b], writes=[pj[1].b])
                        op(DVE, lambda: nc.vector.tensor_tensor(out=hid.t[:, f, :], in0=sg.t[:], in1=pj[1].t[:], op=ALU.mult), reads=[sg.b, pj[1].b], writes=[hid.b])
                    return fn
                for f in range(22):
                    jobs.append((w_fg_d[f], 8 * 128, fgate(f)))
                    jobs.append((w_fu_d[f], 8 * 128, fup(f)))

                def fdown(e):
                    def fn(w):
                        p = pj[2 + e % 2]
                        for f in range(22):
                            op(PE, lambda f=f: nc.tensor.matmul(p.t[:], lhsT=w.t[:, f * 128:(f + 1) * 128], rhs=hid.t[:, f, :], start=(f == 0), stop=(f == 21)),
                               reads=[w.b, hid.b], writes=[p.b])
                        op(DVE, lambda: nc.vector.tensor_tensor(out=xT.t[:, e, :], in0=xT.t[:, e, :], in1=p.t[:], op=ALU.add), reads=[xT.b, p.b], writes=[xT.b])
                    return fn
                for e in range(8):
                    jobs.append((w_fd_d[e], 22 * 128, fdown(e)))

                def post(_):
                    rms_bcast(xT)
                    for c in range(8):
                        op(DVE, lambda c=c: nc.vector.scalar_tensor_tensor(out=xT.t[:, c, :], in0=xT.t[:, c, :], scalar=cst.t[:, CG_FIN + c:CG_FIN + c + 1], in1=rsx.t[:], op0=ALU.mult, op1=ALU.mult),
                           reads=[xT.b, cst.b, rsx.b], writes=[xT.b])
                    for j in range(4):
                        for half in range(2):
                            p = pj[half]
                            for c4 in range(4):
                                c = half * 4 + c4
                                op(PE, lambda c=c, c4=c4, j=j, p=p: nc.tensor.transpose(out=p.t[:, c4 * 128:(c4 + 1) * 128], in_=xT.t[:, c, j * 128:(j + 1) * 128], identity=ident_f.t[:]),
                                   reads=[xT.b, ident_f.b], writes=[p.b])
                            op(ACT, lambda half=half, p=p: nc.scalar.copy(out=ysb.t[:, half * 512:(half + 1) * 512], in_=p.t[:]), reads=[p.b], writes=[ysb.b])
                        r0 = g * 512 + j * 128
                        dma(lambda r0=r0: nc.sync.dma_start(out=y_d[r0:r0 + 128, :], in_=ysb.t[:]), reads=[ysb.b])
                jobs.append((None, 0, post))

            for g in range(Go):
                add_group(g)
            PD = 3
            loaded = {}
            nj = len(jobs)

            def issue_load(i):
                src, n, fn = jobs[i]
                if src is not None:
                    loaded[i] = wblock(src, n)

            widx = [i for i in range(nj) if jobs[i][0] is not None]
            nxt = [0]

            def prefetch_upto(k):
                while nxt[0] < len(widx) and nxt[0] < k:
                    issue_load(widx[nxt[0]])
                    nxt[0] += 1
            wseen = 0
            for i in range(nj):
                src, n, fn = jobs[i]
                if src is not None:
                    prefetch_upto(wseen + 1 + PD)
                    wseen += 1
                    fn(loaded.pop(i))
                else:
                    fn(None)
            for q in DQ:
                if q.n > 0:
                    nc.sync.wait_ge(q.sem, q.n)
        barrier()
        es_cur[0] = es
    return nc


def _wl(W):
    Din, n = W.shape
    C = Din // 128
    return np.ascontiguousarray(W.reshape(C, 128, n).transpose(1, 0, 2).reshape(128, C * n)).astype(np.float32)


def _partner_cols(W, groups):
    P = np.zeros_like(W)
    for st, half in groups:
        P[:, st:st + half] = W[:, st + half:st + 2 * half]
        P[:, st + half:st + 2 * half] = W[:, st:st + half]
    return P


def _inv_freq(rot_dim):
    half = rot_dim // 2
    return np.exp((-math.log(THETA)) * np.arange(half, dtype=np.float32) * np.float32(2.0 / rot_dim)).astype(np.float32)


def prepare_weights(inp):
    w_in = np.asarray(inp["w_in"], np.float32)[0]
    d = {}
    dq = w_in[:, 672:1184]
    dk = w_in[:, 1184:1696]
    dv = w_in[:, 1696:2208]
    grp = [(i * 64, 8) for i in range(4)]
    d["w_dq"] = np.stack([_wl(dq[:, hp * 256:(hp + 1) * 256]) for hp in range(2)])
    d["w_dqP"] = np.stack([_wl(_partner_cols(dq[:, hp * 256:(hp + 1) * 256], grp)) for hp in range(2)])
    d["w_dk"] = np.stack([_wl(dk[:, hp * 256:(hp + 1) * 256]) for hp in range(2)])
    d["w_dkP"] = np.stack([_wl(_partner_cols(dk[:, hp * 256:(hp + 1) * 256], grp)) for hp in range(2)])
    d["w_dv"] = np.stack([_wl(dv[:, hp * 256:(hp + 1) * 256]) for hp in range(2)])
    d["w_kvl"] = _wl(w_in[:, 384:640])
    kr = np.zeros((D, 96), np.float32)
    kr[:, 64:96] = w_in[:, 640:672]
    d["w_kr"] = _wl(kr)
    d["w_krP"] = _wl(_partner_cols(kr, [(64, 16)]))
    d["w_ql"] = _wl(w_in[:, 0:384])
    d["w_ukv"] = _wl(np.asarray(inp["mla_w_ukv"], np.float32)[0])
    uq = np.asarray(inp["mla_w_uq"], np.float32)[0]
    d["w_uq"] = _wl(uq)
    d["w_uqP"] = _wl(_partner_cols(uq, [(h * 96 + 64, 16) for h in range(8)]))
    wg = w_in[:, 2208:4256]
    d["w_g"] = np.stack([_wl(wg[:, e * 128:(e + 1) * 128]) for e in range(16)])
    bm = np.asarray(inp["w_branch_mla"], np.float32)[0]
    bd = np.asarray(inp["w_branch_diff"], np.float32)[0]
    wo = np.asarray(inp["w_out"], np.float32)[0]
    d["w_bm"] = np.stack([_wl(bm[:, e * 128:(e + 1) * 128]) for e in range(8)])
    d["w_bd"] = np.stack([_wl(bd[:, e * 128:(e + 1) * 128]) for e in range(8)])
    d["w_o"] = np.stack([_wl(wo[:, e * 128:(e + 1) * 128]) for e in range(8)])
    fg = np.asarray(inp["w_ffn_gate"], np.float32)[0]
    fu = np.asarray(inp["w_ffn_up"], np.float32)[0]
    fd = np.asarray(inp["w_ffn_down"], np.float32)[0]
    d["w_fg"] = np.stack([_wl(fg[:, f * 128:(f + 1) * 128]) for f in range(22)])
    d["w_fu"] = np.stack([_wl(fu[:, f * 128:(f + 1) * 128]) for f in range(22)])
    d["w_fd"] = np.stack([_wl(fd[:, e * 128:(e + 1) * 128]) for e in range(8)])
    cst = np.zeros((128, NCST), np.float32)

    def pc(v, C):
        return np.asarray(v, np.float32).reshape(C, 128).T

    cst[:, CG_MIX:CG_MIX + 8] = pc(inp["norm_mix_g"][0], 8)
    cst[:, CB_GATE:CB_GATE + 16] = pc(inp["b_gate"][0], 16)
    cst[:, CG_Q:CG_Q + 3] = pc(inp["mla_q_norm_g"][0], 3)
    cst[:, CG_KV:CG_KV + 2] = pc(inp["mla_kv_norm_g"][0], 2)
    cst[:, CG_SUB:CG_SUB + 1] = pc(inp["diff_subln_g"][0], 1)
    cst[:, CG_FFN:CG_FFN + 8] = pc(inp["norm_ffn_g"][0], 8)
    cst[:, CG_FIN:CG_FIN + 8] = pc(inp["norm_final_g"], 8)
    fdiff = _inv_freq(16)
    fmla = _inv_freq(32)
    for p in range(128):
        j = p % 64
        if j < 8:
            cst[p, C_IFD], cst[p, C_SGD] = fdiff[j], -1.0
        elif j < 16:
            cst[p, C_IFD], cst[p, C_SGD] = fdiff[j - 8], 1.0
        if 64 <= p < 80:
            cst[p, C_IFM], cst[p, C_SGM] = fmla[p - 64], -1.0
        elif 80 <= p < 96:
            cst[p, C_IFM], cst[p, C_SGM] = fmla[p - 80], 1.0
    cst[:, C_LQ1:C_LQ1 + 64] = np.asarray(inp["diff_lambda_q1"], np.float32)[0][None, :]
    cst[:, C_LK1:C_LK1 + 64] = np.asarray(inp["diff_lambda_k1"], np.float32)[0][None, :]
    cst[:, C_LQ2:C_LQ2 + 64] = np.asarray(inp["diff_lambda_q2"], np.float32)[0][None, :]
    cst[:, C_LK2:C_LK2 + 64] = np.asarray(inp["diff_lambda_k2"], np.float32)[0][None, :]
    d["cst"] = cst
    d["ident"] = np.eye(128, dtype=np.float32)
    ip = np.zeros((128, 32), np.float32)
    ip[:32, :32] = np.eye(32, dtype=np.float32)
    ip[64:96, :32] = np.eye(32, dtype=np.float32)
    d["ipad"] = ip
    return d


def _mask_for(r):
    m = np.zeros((128, 128), np.float32)
    kk = np.arange(128)[None, :]
    ii = np.arange(32)[:, None]
    m[:32, :] = np.where(kk <= 4 * ii + r, 0.0, NEG)
    m[64:96, :] = m[:32, :]
    return m


_NC_CACHE = {}


def run(inputs, S):
    x = np.asarray(inputs["x"], np.float32)
    pos = np.asarray(inputs["positions"], np.int32)
    B = x.shape[0]
    wd = prepare_weights(inputs)
    if S not in _NC_CACHE:
        _NC_CACHE[S] = build(S)
    nc = _NC_CACHE[S]
    in_maps = []
    for c in range(8):
        b, r = c // 4, c % 4
        m = dict(wd)
        m["x_full"] = np.ascontiguousarray(x[b])
        m["x_own"] = np.ascontiguousarray(x[b, r::4])
        m["posb_full"] = np.ascontiguousarray(np.broadcast_to(pos[b][None, :], (128, S)))
        m["posb_own"] = np.ascontiguousarray(np.broadcast_to(pos[b, r::4][None, :], (128, S // 4)))
        m["maskT"] = _mask_for(r)
        in_maps.append(m)
    res = run_bass_kernel_spmd(nc, in_maps, core_ids=list(range(8)))
    out = np.zeros((B, S, D), np.float32)
    for c in range(8):
        b, r = c // 4, c % 4
        out[b, r::4] = res.results[c]["y"]
    return out


def kernel(**inputs):
    return run(inputs, 8192)
```
